# Optimizing a Trainium2 kernel written in Bass

```python
import math
import jax, jax.numpy as jnp
from jax import lax
import numpy as np

D_MODEL = 1024
BATCH = 2
SEQ = 8192
DEPTH = 1

D_PLE = 256
M_HEADS = 4
M_QK = 128
M_V = 256
M_WIDTH = M_HEADS * M_V
M_CHUNK = 64
CONV_W = 4
A_HEADS = 8
A_DH = 64
A_WIDTH = A_HEADS * A_DH
IDX_HEADS = 8
IDX_DH = 64
TOPK_MAX = 256
Q_BLOCK = 128
REL_BUCKETS = 32
REL_MAX_DIST = 128
D_MIX = M_WIDTH + A_WIDTH
EPS = 1e-6

PROJ_SIZES = [
    M_HEADS * M_QK,
    M_HEADS * M_QK,
    M_WIDTH,
    M_WIDTH,
    M_WIDTH,
    2 * M_HEADS,
    A_WIDTH,
    A_WIDTH,
    A_WIDTH,
    A_WIDTH,
    IDX_HEADS * IDX_DH,
    IDX_DH,
    IDX_HEADS,
]
PROJ_TOTAL = int(sum(PROJ_SIZES))
SPLIT_POINTS = [int(v) for v in np.cumsum(PROJ_SIZES)[:-1]]

kernel_name = "hybrid_mlstm_dsa_parallel_heads"


def rmsnorm(x, g):
    xf = x.astype(jnp.float32)
    y = xf * lax.rsqrt(jnp.mean(xf * xf, axis=-1, keepdims=True) + EPS)
    return (y * g.astype(jnp.float32)).astype(x.dtype)


def causal_dwconv(x, w):
    S = x.shape[1]
    xp = jnp.pad(x, ((0, 0), (CONV_W - 1, 0), (0, 0)))
    y = xp[:, 0:S] * w[0]
    for j in range(1, CONV_W):
        y = y + xp[:, j:j + S] * w[j]
    return y


def mlstm_chunkwise(q, k, v, log_i, log_f):
    f32 = jnp.float32
    B, S, H, Dk = q.shape
    Dv = v.shape[-1]
    L = M_CHUNK
    NC = S // L

    def to_chunks(a):
        return a.reshape((B, NC, L) + a.shape[2:]).swapaxes(0, 1)

    qc = to_chunks(q.astype(f32))
    kc = to_chunks(k.astype(f32) * (Dk ** -0.5))
    vc = to_chunks(v.astype(f32))
    ic = to_chunks(log_i.astype(f32))
    fc = to_chunks(log_f.astype(f32))
    causal = jnp.tril(jnp.ones((L, L), dtype=bool))

    def step(carry, inp):
        C, n, m = carry
        q_, k_, v_, li, lf = inp
        b = jnp.cumsum(lf, axis=1).transpose(0, 2, 1)
        li_h = li.transpose(0, 2, 1)
        D = b[:, :, :, None] - b[:, :, None, :] + li_h[:, :, None, :]
        D = jnp.where(causal, D, -jnp.inf)
        inter = b + m[:, :, None]
        m_t = jnp.maximum(inter, D.max(-1))
        s = jnp.einsum('blhd,bshd->bhls', q_, k_) * jnp.exp(D - m_t[..., None])
        w_inter = jnp.exp(inter - m_t)
        num = (jnp.einsum('bhls,bshv->blhv', s, v_)
               + w_inter.transpose(0, 2, 1)[..., None] * jnp.einsum('blhd,bhdv->blhv', q_, C))
        den = s.sum(-1) + w_inter * jnp.einsum('blhd,bhd->bhl', q_, n)
        den = jnp.maximum(jnp.abs(den), jnp.exp(-m_t))
        h = num / den.transpose(0, 2, 1)[..., None]
        b_L = b[:, :, -1]
        g = b_L[:, :, None] - b + li_h
        m_new = jnp.maximum(b_L + m, g.max(-1))
        wk = jnp.exp(g - m_new[..., None])
        decay = jnp.exp(b_L + m - m_new)
        C_new = decay[..., None, None] * C + jnp.einsum('bhs,bshd,bshv->bhdv', wk, k_, v_)
        n_new = decay[..., None] * n + jnp.einsum('bhs,bshd->bhd', wk, k_)
        return (C_new, n_new, m_new), h

    init = (jnp.zeros((B, H, Dk, Dv), f32), jnp.zeros((B, H, Dk), f32), jnp.zeros((B, H), f32))
    _, hs = lax.scan(step, init, (qc, kc, vc, ic, fc))
    return hs.swapaxes(0, 1).reshape(B, S, H, Dv)


def t5_bucket(dist):
    max_exact = REL_BUCKETS // 2
    large = max_exact + (jnp.log(jnp.maximum(dist, 1).astype(jnp.float32) / max_exact)
                         / math.log(REL_MAX_DIST / max_exact) * (REL_BUCKETS - max_exact)).astype(jnp.int32)
    large = jnp.minimum(large, REL_BUCKETS - 1)
    return jnp.where(dist < max_exact, dist, large)


def dsa_attention(q, k, v, iq, ik, iw, rel_bias):
    B, S, H, dh = q.shape
    topk = min(TOPK_MAX, S // 4)
    NB = S // Q_BLOCK
    key_pos = jnp.arange(S)
    iq = iq * (IDX_DH ** -0.5)
    iw = iw * (IDX_HEADS ** -0.5)

    def blockify(a):
        return a.reshape((B, NB, Q_BLOCK) + a.shape[2:]).swapaxes(0, 1)

    def one_block(args):
        qb, iqb, iwb, bi = args
        t = bi * Q_BLOCK + jnp.arange(Q_BLOCK)
        sc = jax.nn.relu(jnp.einsum('bthd,bsd->bths', iqb, ik))
        score = jnp.einsum('bths,bth->bts', sc, iwb).astype(jnp.float32)
        causal = key_pos[None, :] <= t[:, None]
        score = jnp.where(causal[None], score, -jnp.inf)
        _, idx = lax.top_k(score, topk)
        k_sel = jax.vmap(lambda kb, ib: kb[ib])(k, idx)
        v_sel = jax.vmap(lambda vb, ib: vb[ib])(v, idx)
        valid = idx <= t[None, :, None]
        bucket = t5_bucket(jnp.maximum(t[None, :, None] - idx, 0))
        bias = rel_bias[bucket].astype(jnp.float32).transpose(0, 1, 3, 2)
        logits = jnp.einsum('bthd,btkhd->bthk', qb, k_sel).astype(jnp.float32) * (dh ** -0.5) + bias
        logits = jnp.where(valid[:, :, None, :], logits, -jnp.inf)
        probs = jax.nn.softmax(logits, axis=-1).astype(v.dtype)
        return jnp.einsum('bthk,btkhd->bthd', probs, v_sel)

    out = lax.map(one_block, (blockify(q), blockify(iq), blockify(iw), jnp.arange(NB)))
    return out.swapaxes(0, 1).reshape(B, S, H, dh)


def setup_inputs(seed: int = 0) -> dict:
    key = jax.random.key(seed)
    ks = jax.random.split(key, 16)
    f32 = jnp.float32
    x = jax.random.normal(ks[0], (BATCH, SEQ, D_MODEL), f32)
    p = jax.random.normal(ks[1], (DEPTH, BATCH, SEQ, D_PLE), f32)
    w_in = jax.random.normal(ks[2], (DEPTH, D_MODEL, PROJ_TOTAL), f32) * D_MODEL ** -0.5
    i_bias = 0.1 * jax.random.normal(ks[3], (DEPTH, M_HEADS), f32)
    f_bias = jnp.linspace(3.0, 6.0, M_HEADS, dtype=f32)[None] + 0.1 * jax.random.normal(ks[4], (DEPTH, M_HEADS), f32)
    b_gate = jnp.concatenate([i_bias, f_bias], axis=-1)
    conv_w = jax.random.normal(ks[5], (DEPTH, CONV_W, 2 * M_HEADS * M_QK), f32) * CONV_W ** -0.5
    w_out = jax.random.normal(ks[6], (DEPTH, D_MIX, D_MODEL), f32) * D_MIX ** -0.5
    norm_in = 1.0 + 0.1 * jax.random.normal(ks[7], (DEPTH, D_MODEL), f32)
    m_norm = 1.0 + 0.1 * jax.random.normal(ks[8], (DEPTH, M_WIDTH), f32)
    rel_bias = 0.5 * jax.random.normal(ks[9], (REL_BUCKETS, A_HEADS), f32)
    w_ple = jax.random.normal(ks[10], (DEPTH, D_PLE, D_MODEL), f32) * D_PLE ** -0.5
    w_ple_gate = jax.random.normal(ks[11], (DEPTH, D_MODEL, D_MODEL), f32) * D_MODEL ** -0.5
    norm_final = 1.0 + 0.1 * jax.random.normal(ks[12], (D_MODEL,), f32)
    return {"x": x, "p": p, "w_in": w_in, "b_gate": b_gate, "conv_w": conv_w, "w_out": w_out,
            "norm_in": norm_in, "m_norm": m_norm, "rel_bias": rel_bias, "w_ple": w_ple,
            "w_ple_gate": w_ple_gate, "norm_final": norm_final}


def reference(x, p, w_in, b_gate, conv_w, w_out, norm_in, m_norm, rel_bias, w_ple, w_ple_gate, norm_final):
    B, S, _ = x.shape
    h = x
    for i in range(DEPTH):
        u = rmsnorm(h, norm_in[i])
        proj = u @ w_in[i]
        (m_q, m_k, m_v, m_o, m_z, m_if, a_q, a_k, a_v, a_z,
         ix_q, ix_k, ix_w) = jnp.split(proj, SPLIT_POINTS, axis=-1)

        qk = jax.nn.silu(causal_dwconv(jnp.concatenate([m_q, m_k], axis=-1), conv_w[i]))
        mq, mk = jnp.split(qk, 2, axis=-1)
        gates = (m_if + b_gate[i]).astype(jnp.float32)
        log_i = gates[..., :M_HEADS]
        log_f = jax.nn.log_sigmoid(gates[..., M_HEADS:])
        hm = mlstm_chunkwise(mq.reshape(B, S, M_HEADS, M_QK), mk.reshape(B, S, M_HEADS, M_QK),
                             m_v.reshape(B, S, M_HEADS, M_V), log_i, log_f)
        hm = hm * lax.rsqrt(jnp.mean(hm * hm, axis=-1, keepdims=True) + EPS)
        hm = hm.reshape(B, S, M_WIDTH) * m_norm[i].astype(jnp.float32)
        hm = jax.nn.sigmoid(m_o) * hm.astype(x.dtype) * jax.nn.silu(m_z)

        ha = dsa_attention(a_q.reshape(B, S, A_HEADS, A_DH), a_k.reshape(B, S, A_HEADS, A_DH),
                           a_v.reshape(B, S, A_HEADS, A_DH), ix_q.reshape(B, S, IDX_HEADS, IDX_DH),
                           ix_k, ix_w, rel_bias)
        ha = ha.reshape(B, S, A_WIDTH) * jax.nn.silu(a_z)

        h = h + jnp.concatenate([hm, ha], axis=-1) @ w_out[i]

        h = h + (p[i] @ w_ple[i]) * jax.nn.sigmoid(h @ w_ple_gate[i])
    return rmsnorm(h, norm_final)
```

```python
import math
QLIM = 16
from contextlib import ExitStack
import numpy as np
import concourse.bass as bass
import concourse.mybir as mybir
from concourse.bass_utils import run_bass_kernel_spmd

F32 = mybir.dt.float32
BF16 = mybir.dt.bfloat16
ALU = mybir.AluOpType
AF = mybir.ActivationFunctionType

DEBUG = False
NBIS = 34
ACT_FRAC = 0.54
BIG = 1.0e9


def _dsize(dt):
    if dt == F32:
        return 4
    if dt == BF16:
        return 2
    s = str(dt)
    if '32' in s:
        return 4
    if '16' in s:
        return 2
    if '64' in s:
        return 8
    return 1


class Op:
    __slots__ = ('eng', 'fn', 'deps', 'sig', 'cnt', 'dma', 'slot', 'dcnt', 'idx')


class Prog:
    ENGS = ['sp', 'act', 'dve', 'pool', 'pe']
    NSLOT = 16
    SAME_SYNC = True

    def __init__(self, nc):
        self.nc = nc
        self.ops = {e: [] for e in self.ENGS}
        self.track = {}
        self.slot_last = {}
        self.rr = {e: 0 for e in self.ENGS}
        self.nops = 0

    def region(self, ap):
        t = ap.tensor
        sp = str(ap.space)
        dims = ap.ap
        es = _dsize(ap.dtype)
        off = int(ap.offset)
        if 'SB' in sp or 'PSUM' in sp:
            pstep = dims[0][0]
            pcnt = dims[0][1]
            if pstep == 0:
                plo = 0
                foff = off
            else:
                plo = off // pstep
                foff = off - plo * pstep
            lo = foff
            hi = foff
            for st, c in dims[1:]:
                if st >= 0:
                    hi += st * (c - 1)
                else:
                    lo += st * (c - 1)
            if 'PSUM' in sp:
                return ((sp, t.name), 0, 128, 0, 2048)
            return ((sp, t.name), plo, plo + pcnt, lo * es, (hi + 1) * es)
        lo = off
        hi = off
        for st, c in dims:
            if st >= 0:
                hi += st * (c - 1)
            else:
                lo += st * (c - 1)
        return ((sp, t.name), 0, 1, lo * es, (hi + 1) * es)

    def add(self, eng, fn, reads=(), writes=(), dma=False):
        op = Op()
        op.eng = eng
        op.fn = fn
        op.sig = False
        op.cnt = 0
        op.dma = dma
        op.slot = None
        op.dcnt = 0
        op.idx = self.nops
        self.nops += 1
        deps = set()
        rregs = [self.region(a) for a in reads]
        wregs = [self.region(a) for a in writes]
        for (key, plo, phi, lo, hi) in rregs:
            for e in self.track.get(key, ()):
                if e[5] and e[0] < phi and plo < e[1] and e[2] < hi and lo < e[3]:
                    deps.add(e[4])
        for (key, plo, phi, lo, hi) in wregs:
            for e in self.track.get(key, ()):
                if e[0] < phi and plo < e[1] and e[2] < hi and lo < e[3]:
                    deps.add(e[4])
        for (key, plo, phi, lo, hi) in wregs:
            lst = self.track.setdefault(key, [])
            lst[:] = [e for e in lst if not (plo <= e[0] and e[1] <= phi and lo <= e[2] and e[3] <= hi)]
            lst.append((plo, phi, lo, hi, op, True))
        for (key, plo, phi, lo, hi) in rregs:
            self.track.setdefault(key, []).append((plo, phi, lo, hi, op, False))
        if dma:
            s = self.rr[eng] % self.NSLOT
            self.rr[eng] += 1
            prev = self.slot_last.get((eng, s))
            op.slot = s
            op.dcnt = (prev.dcnt if prev is not None else 0) + 1
            if prev is not None:
                deps.add(prev)
            self.slot_last[(eng, s)] = op
        deps.discard(op)
        op.deps = list(deps)
        self.ops[eng].append(op)
        return op

    def _skip(self, d, ename):
        return d.eng == ename and (ename in ('pe', 'sp') or not self.SAME_SYNC)

    def prepare(self):
        for e in self.ENGS:
            for op in self.ops[e]:
                for d in op.deps:
                    if d.dma or self._skip(d, op.eng):
                        continue
                    d.sig = True
        for e in self.ENGS:
            c = 0
            for op in self.ops[e]:
                if op.sig:
                    c += 1
                    op.cnt = c

    def run(self, ename, eng, sems, dsems):
        waited = {}
        for op in self.ops[ename]:
            for d in sorted(op.deps, key=lambda o: o.idx):
                if d.dma:
                    key = ('d', d.eng, d.slot)
                    val = 16 * d.dcnt
                    sem = dsems[d.eng][d.slot]
                else:
                    if self._skip(d, ename):
                        continue
                    key = ('e', d.eng)
                    val = d.cnt
                    sem = sems[d.eng]
                if waited.get(key, 0) >= val:
                    continue
                waited[key] = val
                eng.wait_ge(sem, val)
            ins = op.fn(eng)
            if op.dma:
                ins.then_inc(dsems[ename][op.slot], 16)
            elif op.sig:
                ins.then_inc(sems[ename], 1)

    def final_waits(self, eng, dsems):
        for (e, s), op in self.slot_last.items():
            eng.wait_ge(dsems[e][s], 16 * op.dcnt)


D = 1024
NPOS = 16
NRT = 64
NQT = 16
PT = 6736
C_MQ, C_MK, C_MV, C_MO, C_MZ, C_IF, C_AQ, C_AK, C_AV, C_AZ, C_IQ, C_IK, C_IW = (
    0, 512, 1024, 2048, 3072, 4096, 4104, 4616, 5128, 5640, 6152, 6664, 6728)
LN_DK = math.log(128.0 ** -0.5)
SLABS = [(0, 512), (512, 512), (1024, 512), (1536, 512), (2048, 512), (2560, 512), (3072, 512), (3584, 512),
         (4096, 8), (4104, 512), (4616, 512), (5128, 512), (5640, 512), (6152, 512), (6664, 64), (6728, 8)]
(S_QK0, S_QK1, S_MV0, S_MV1, S_MO0, S_MO1, S_MZ0, S_MZ1, S_IF, S_AQ, S_AK, S_AV, S_AZ, S_IQ, S_IK, S_IW) = range(16)


def t5_bucket_np(dist):
    max_exact = 16
    d = np.maximum(dist, 1).astype(np.float32)
    large = max_exact + (np.log(d / max_exact) / math.log(128 / max_exact) * (32 - max_exact)).astype(np.int32)
    large = np.minimum(large, 31)
    return np.where(dist < max_exact, dist, large)


def make_consts():
    c = {}
    c['ident'] = np.eye(128, dtype=np.float32)
    s = np.arange(128)[:, None]
    t = np.arange(128)[None, :]
    c['tri'] = (s <= t).astype(np.float32)
    c['ones'] = np.ones((128, 128), np.float32)
    c['cneg'] = np.where(t <= s, 0.0, -BIG).astype(np.float32)
    oh = np.zeros((128, 384), np.float32)
    d = np.arange(384) - 128
    bk = t5_bucket_np(np.maximum(d, 0))
    for i in range(384):
        if d[i] >= 0:
            oh[bk[i], i] = 1.0
    c['oh'] = oh
    return np.concatenate([c['ident'], c['tri'], c['ones'], c['cneg'], c['oh']], axis=1).astype(np.float32)


CO_ID, CO_TRI, CO_ONES, CO_CNEG, CO_OH, CO_END = 0, 128, 256, 384, 512, 896


def build_program(debug=False):
    nc = bass.Bass("TRN2", target_bir_lowering=False)
    EI = "ExternalInput"
    xk = nc.dram_tensor("xk", [8192, D], F32, kind=EI).ap()
    pq = nc.dram_tensor("pq", [2048, 256], F32, kind=EI).ap()
    vmask = nc.dram_tensor("vmask", [1, 1536], F32, kind=EI).ap()
    w_in = nc.dram_tensor("w_in", [D, PT], F32, kind=EI).ap()
    b_gate = nc.dram_tensor("b_gate", [1, 8], F32, kind=EI).ap()
    conv_w = nc.dram_tensor("conv_w", [128, 32], F32, kind=EI).ap()
    w_out = nc.dram_tensor("w_out", [1536, D], F32, kind=EI).ap()
    norm_in = nc.dram_tensor("norm_in", [128, 8], F32, kind=EI).ap()
    m_norm = nc.dram_tensor("m_norm", [1, D], F32, kind=EI).ap()
    rel_bias = nc.dram_tensor("rel_bias", [32, 8], F32, kind=EI).ap()
    w_ple = nc.dram_tensor("w_ple", [256, D], F32, kind=EI).ap()
    w_gate = nc.dram_tensor("w_ple_gate", [D, D], F32, kind=EI).ap()
    norm_final = nc.dram_tensor("norm_final", [1, D], F32, kind=EI).ap()
    consts = nc.dram_tensor("consts", [128, CO_END], F32, kind=EI).ap()
    out = nc.dram_tensor("out", [2048, D], F32, kind="ExternalOutput").ap()
    SK = "ExternalOutput" if debug else "Internal"
    W_s = nc.dram_tensor("W_s", [len(SLABS), 128, 8, 512], BF16, kind="Internal").ap()
    KT_s = nc.dram_tensor("KT_s", [4, 128, 8192], BF16, kind=SK).ap()
    VA_s = nc.dram_tensor("VA_s", [64, 128, 520], BF16, kind=SK).ap()
    HM_s = nc.dram_tensor("HM_s", [16, 128, 1024], BF16, kind=SK).ap()
    AQ_s = nc.dram_tensor("AQ_s", [128, 4, 2048], BF16, kind=SK).ap()
    IQ_s = nc.dram_tensor("IQ_s", [128, 4, 2048], BF16, kind=SK).ap()
    GA_s = nc.dram_tensor("GA_s", [16, 128, 512], BF16, kind=SK).ap()
    HA_s = nc.dram_tensor("HA_s", [16, 128, 512], BF16, kind=SK).ap()
    WQ_s = nc.dram_tensor("WQ_s", [16, 128, 8], F32, kind=SK).ap()
    FV_s = nc.dram_tensor("FV_s", [384, 8], F32, kind=SK).ap()
    SC_s = nc.dram_tensor("SC_s", [16, 128, 8192], BF16, kind=SK).ap() if debug else None

    es = ExitStack()
    with es:
        def sb(name, shape, dt):
            return es.enter_context(nc.sbuf_tensor(name, shape, dt))

        ARENA = 166 * 1024
        arena = sb("arena", [128, ARENA // 2], BF16)
        bump = [0]

        def carve(shape, dt):
            n = 1
            for s_ in shape[1:]:
                n *= s_
            nb = n * _dsize(dt)
            nb = (nb + 63) // 64 * 64
            o = bump[0]
            bump[0] += nb
            assert bump[0] <= ARENA, (bump[0], ARENA)
            v = arena[0:shape[0], o // 2:(o + nb) // 2]
            if dt == F32:
                v = v.bitcast(F32)
            v = v[:, 0:n]
            if len(shape) == 3:
                v = v.rearrange("p (a b) -> p a b", b=shape[2])
            elif len(shape) == 4:
                v = v.rearrange("p (a b c) -> p a b c", b=shape[2], c=shape[3])
            return v

        cst = sb("cst", [128, CO_END], F32)
        identb = sb("identb", [128, 128], BF16)
        c01b = sb("c01b", [128, 128], BF16)
        negI = sb("negI", [128, 128], BF16)
        ikT = sb("ikT", [128, 8192], BF16)
        g8 = sb("g8", [128, 8], F32)
        cw = sb("cw", [128, 4, 8], F32)
        bgb = sb("bgb", [128, 8], F32)
        relb = sb("relb", [32, 8], F32)
        rel31 = sb("rel31", [32, 8], F32)
        sm = sb("sm", [128, 256], F32)
        pss = [es.enter_context(nc.psum_tensor(f"ps{i}", [128, 512], F32)) for i in range(8)]
        sems = {e: es.enter_context(nc.semaphore("s_" + e)) for e in Prog.ENGS}
        dsems = {'sp': [es.enter_context(nc.semaphore(f"d_sp{i}")) for i in range(Prog.NSLOT)]}
        P = Prog(nc)
        ident = cst[:, CO_ID:CO_ID + 128]
        triF = cst[:, CO_TRI:CO_TRI + 128]
        onesF = cst[:, CO_ONES:CO_ONES + 128]
        cnegF = cst[:, CO_CNEG:CO_CNEG + 128]
        ohF = cst[0:32, CO_OH:CO_OH + 384]

        def dma(o, i):
            P.add('sp', lambda e: e.dma_start(out=o, in_=i), reads=[i], writes=[o], dma=True)

        def psb(i, shape):
            v = pss[i][:].bitcast(BF16)
            n = 1
            for s_ in shape[1:]:
                n *= s_
            v = v[0:shape[0], 0:n]
            if len(shape) == 3:
                v = v.rearrange("p (a b) -> p a b", b=shape[2])
            return v

        def tsc(eng, o, i, s1, s2, op0, op1=None, reads=None):
            rd = [i] + [s for s in (s1, s2) if not isinstance(s, (int, float)) and s is not None]
            if op1 is None:
                P.add(eng, lambda e: e.tensor_scalar(o, i, s1, None, op0), reads=rd, writes=[o])
            else:
                P.add(eng, lambda e: e.tensor_scalar(o, i, s1, s2, op0, op1), reads=rd, writes=[o])

        def tt(eng, o, a, b, op):
            P.add(eng, lambda e: e.tensor_tensor(o, a, b, op), reads=[a, b], writes=[o])

        def stt(o, a, s, b, op0, op1):
            rd = [a, b] + ([] if isinstance(s, (int, float)) else [s])
            P.add('dve', lambda e: e.scalar_tensor_tensor(o, a, s, b, op0, op1), reads=rd, writes=[o])

        def act(o, i, func, scale=None, bias=None, accum=None):
            kw = {}
            rd = [i]
            wr = [o]
            if scale is not None:
                kw['scale'] = scale
                if not isinstance(scale, (int, float)):
                    rd.append(scale)
            if bias is not None:
                kw['bias'] = bias
                if not isinstance(bias, (int, float)):
                    rd.append(bias)
            if accum is not None:
                kw['accum_out'] = accum
                wr.append(accum)
            P.add('act', lambda e: e.activation(out=o, in_=i, func=func, **kw), reads=rd, writes=wr)

        def cp(eng, o, i):
            if eng == 'act':
                P.add('act', lambda e: e.copy(o, i), reads=[i], writes=[o])
            else:
                P.add(eng, lambda e: e.tensor_copy(o, i), reads=[i], writes=[o])

        def mm(o, pairs, reads_extra=()):
            rd = []
            for l, r in pairs:
                rd += [l, r]

            def fn(e):
                ins = None
                n = len(pairs)
                for i_, (l, r) in enumerate(pairs):
                    ins = e.matmul(o, lhsT=l, rhs=r, start=(i_ == 0), stop=(i_ == n - 1))
                return ins
            P.add('pe', fn, reads=rd, writes=[o])

        def tr(o, i, idt):
            P.add('pe', lambda e: e.transpose(o, i, idt), reads=[i, idt], writes=[o])

        def memset(eng, o, v):
            P.add(eng, lambda e: e.memset(o, v), writes=[o])

        dma(cst[:], consts)
        dma(g8[:], norm_in)
        dma(cw[:], conv_w.rearrange("p (j c) -> p j c", c=8))
        dma(bgb[:], b_gate.partition_broadcast(128))
        dma(relb[:], rel_bias)
        dma(rel31[:], rel_bias[31:32, :].partition_broadcast(32))
        cp('dve', identb[:], ident)
        cp('dve', c01b[:], triF)
        tsc('dve', negI[:], ident, -30000.0, None, ALU.mult)
        tt('dve', relb[:], relb[:], rel31[:], ALU.subtract)
        fvt = sm[:, 0:24].rearrange("p (a b) -> p a b", b=8)
        for i3 in range(3):
            mm(pss[7][:, i3 * 8:(i3 + 1) * 8], [(ohF[:, i3 * 128:(i3 + 1) * 128], relb[:])])
        cp('dve', sm[:, 0:24], pss[7][:, 0:24])
        dma(FV_s.rearrange("(a p) h -> p a h", p=128), fvt)

        bump[0] = ARENA - 24 * 1024 - 64
        wst = carve([128, 8, 512], F32)
        wsb = carve([128, 8, 512], BF16)
        w_in_v = w_in.rearrange("(k p) c -> p k c", p=128)

        def conv_slab(sl):
            c0, ncl = SLABS[sl]
            dma(wst[:, :, 0:ncl], w_in_v[:, :, c0:c0 + ncl])
            for k in range(8):
                tsc('dve', wsb[:, k, 0:ncl], wst[:, k, 0:ncl], g8[:, k:k + 1], None, ALU.mult)
            if ncl == 64:
                for k in range(8):
                    cp('pool', wsb[:, k, 64:128], wsb[:, k, 0:64])
            dma(W_s[sl], wsb[:])
        converted = set()

        def ensure_conv(sl):
            if sl not in converted:
                converted.add(sl)
                conv_slab(sl)

        def gen0_rest():
            for sl in (S_AQ, S_IQ, S_MO0, S_MO1, S_MZ0, S_MZ1, S_AZ, S_IW):
                ensure_conv(sl)
                yield

        bump[0] = 0
        xt = [carve([128, 1024], F32) for _ in range(2)]
        junkb = carve([128, 1024], BF16)
        xnB = [carve([128, 1024], BF16) for _ in range(2)]
        uTB = [carve([128, 8, 512], BF16) for _ in range(2)]
        wsl = [carve([128, 8, 512], BF16) for _ in range(3)]
        qkpre = carve([128, 8, 516], F32)
        ctmp = [carve([128, 512], F32) for _ in range(2)]
        qkTB = [carve([128, 8, 512], BF16) for _ in range(2)]
        VpB = [carve([128, 4, 4, 257], BF16) for _ in range(2)]
        kst = carve([128, 4, 512], BF16)
        VAst = carve([128, 4, 8, 65], BF16)
        gtB = [carve([128, 4, 8], F32) for _ in range(2)]
        KhatB = [carve([128, 128], BF16) for _ in range(4)]
        mnb = carve([128, 1024], F32)
        Cst = carve([128, 4, 257], F32)
        Cb = carve([128, 4, 257], BF16)
        qst = carve([128, 4, 512], BF16)
        GmB = [carve([128, 4, 1024], BF16) for _ in range(2)]
        slz = [carve([128, 512], F32) for _ in range(2)]
        hh = carve([128, 4, 256], F32)
        hmg = carve([128, 1024], BF16)
        gast = carve([128, 4, 512], BF16)
        wqst = carve([128, 4, 8], F32)
        ATB = [carve([128, 128], BF16) for _ in range(4)]
        print("phase A arena bytes", bump[0])
        dma(mnb[:], m_norm.partition_broadcast(128))
        memset('pool', Cst[:], 0.0)
        memset('pool', Cb[:], 0.0)
        memset('pool', qkpre[:, :, 0:3], 0.0)
        for i_ in range(2):
            memset('pool', VpB[i_][:], 1.0)
        memset('pool', VAst[:], 1.0)
        hs = sm[:, 72:76]
        rn = sm[:, 76:80]
        d1 = sm[:, 80:84]
        scv = sm[:, 84:88]
        wslot = [0]

        def load_slab(sid):
            ensure_conv(sid)
            w = wsl[wslot[0] % 3]
            wslot[0] += 1
            dma(w[:], W_s[sid])
            return w

        pbank = [0]

        def nextbank():
            pbank[0] += 1
            return pss[1 + pbank[0] % 2]

        def xload(r):
            dma(xt[r % 2][:], xk[r * 128:(r + 1) * 128, :])

        def genP(p):
            isq = (p % 4 == 3)
            qi = p // 4
            uT = uTB[p % 2]
            qkT = qkTB[p % 2]
            Vp = VpB[p % 2]
            gt = gtB[p % 2]
            Gm = GmB[p % 2]
            def j_qk(half):
                def f(w):
                    for c4 in range(4):
                        ct = half * 4 + c4
                        pb = nextbank()
                        mm(pb[:], [(w[:, k, c4 * 128:(c4 + 1) * 128], uT[:, k, :]) for k in range(8)])
                        cp('act', qkpre[:, ct, 3:515], pb[:])
                        tm_ = ctmp[ct % 2]
                        tsc('dve', tm_[:], qkpre[:, ct, 0:512], cw[:, 0, ct:ct + 1], None, ALU.mult)
                        for j in range(1, 4):
                            stt(tm_[:], qkpre[:, ct, j:j + 512], cw[:, j, ct:ct + 1], tm_[:], ALU.mult, ALU.add)
                        act(qkT[:, ct, :], tm_[:], AF.Silu)
                        cp('pool', qkpre[:, ct, 0:3], qkpre[:, ct, 512:515])
                        yield
                return f

            def j_fm4(dst, scale, after):
                def f(w):
                    for pr in range(4):
                        pb = nextbank()
                        mm(pb[:], [(w[:, k, pr * 128:(pr + 1) * 128], uT[:, k, :]) for k in range(8)])
                        if scale is None:
                            cp('act', dst[:, pr, :], pb[:])
                        else:
                            act(dst[:, pr, :], pb[:], AF.Copy, scale=scale)
                        yield
                    after()
                return f

            def j_ik(w):
                pb = nextbank()
                mm(pb[:], [(w[:, k, 0:128], uT[:, k, :]) for k in range(8)])
                cp('act', ikT[:, p * 512:(p + 1) * 512], pb[:])
                yield

            def j_mv(half):
                def f(w):
                    for rt in range(4):
                        pb = nextbank()
                        mm(pb[:], [(uT[:, k, rt * 128:(rt + 1) * 128], w[:, k, :]) for k in range(8)])
                        cp('act', Vp[:, rt, 2 * half:2 * half + 2, 0:256], pb[:].rearrange("p (a b) -> p a b", b=256))
                        yield
                return f

            def j_av(w):
                for rt in range(4):
                    pb = nextbank()
                    mm(pb[:], [(uT[:, k, rt * 128:(rt + 1) * 128], w[:, k, :]) for k in range(8)])
                    cp('act', VAst[:, rt, :, 0:64], pb[:].rearrange("p (a b) -> p a b", b=64))
                    yield
                dma(VA_s[4 * p:4 * p + 4].rearrange("r t c -> t r c"), VAst[:].rearrange("p r a b -> p r (a b)"))

            def j_if(w):
                for rt in range(4):
                    pb = nextbank()
                    mm(pb[:, 0:8], [(uT[:, k, rt * 128:(rt + 1) * 128], w[:, k, 0:8]) for k in range(8)])
                    tt('dve', gt[:, rt, :], pb[:, 0:8], bgb[:], ALU.add)
                yield

            def j_mo(half):
                def f(w):
                    for rt in range(4):
                        pb = nextbank()
                        mm(pb[:], [(uT[:, k, rt * 128:(rt + 1) * 128], w[:, k, :]) for k in range(8)])
                        act(Gm[:, rt, half * 512:(half + 1) * 512], pb[:], AF.Sigmoid)
                        yield
                return f

            def j_mz(half):
                def f(w):
                    for rt in range(4):
                        pb = nextbank()
                        mm(pb[:], [(uT[:, k, rt * 128:(rt + 1) * 128], w[:, k, :]) for k in range(8)])
                        z_ = slz[rt % 2]
                        act(z_[:], pb[:], AF.Silu)
                        tt('pool', Gm[:, rt, half * 512:(half + 1) * 512], Gm[:, rt, half * 512:(half + 1) * 512], z_[:], ALU.mult)
                        yield
                return f

            def j_az(w):
                for rt in range(4):
                    pb = nextbank()
                    mm(pb[:], [(uT[:, k, rt * 128:(rt + 1) * 128], w[:, k, :]) for k in range(8)])
                    act(gast[:, rt, :], pb[:], AF.Silu)
                    yield
                dma(GA_s[4 * qi:4 * qi + 4].rearrange("r t c -> t r c"), gast[:])

            def j_iw(w):
                for rt in range(4):
                    pb = nextbank()
                    mm(pb[:, 0:8], [(uT[:, k, rt * 128:(rt + 1) * 128], w[:, k, 0:8]) for k in range(8)])
                    act(wqst[:, rt, :], pb[:, 0:8], AF.Copy, scale=1.0 / (8.0 * math.sqrt(8.0)))
                dma(WQ_s[4 * qi:4 * qi + 4].rearrange("r t c -> t r c"), wqst[:])
                yield

            jobs = [(S_QK0, j_qk(0)), (S_QK1, j_qk(1)),
                    (S_AK, j_fm4(kst, None, lambda: dma(KT_s[:, :, p * 512:(p + 1) * 512].rearrange("a d s -> d a s"), kst[:]))),
                    (S_IK, j_ik)]
            if isq:
                jobs += [(S_AQ, j_fm4(qst, 0.125, lambda: dma(AQ_s[:, :, qi * 512:(qi + 1) * 512], qst[:]))),
                         (S_IQ, j_fm4(qst, None, lambda: dma(IQ_s[:, :, qi * 512:(qi + 1) * 512], qst[:])))]
            jobs += [(S_MV0, j_mv(0)), (S_MV1, j_mv(1)), (S_AV, j_av), (S_IF, j_if)]
            if isq:
                jobs += [(S_MO0, j_mo(0)), (S_MO1, j_mo(1)), (S_MZ0, j_mz(0)), (S_MZ1, j_mz(1)), (S_AZ, j_az), (S_IW, j_iw)]
            slabs = [load_slab(jobs[0][0]), load_slab(jobs[1][0])]
            for ji, (sid, jf) in enumerate(jobs):
                if ji + 2 < len(jobs):
                    slabs.append(load_slab(jobs[ji + 2][0]))
                for _ in jf(slabs[ji]):
                    yield

        def genN(p):
            uT = uTB[p % 2]
            if p == 0:
                xload(0)
            for rt in range(4):
                r = 4 * p + rt
                if r + 1 < NRT:
                    xload(r + 1)
                x_ = xt[r % 2]
                xn = xnB[r % 2]
                ss = sm[:, 32 + (r % 2):33 + (r % 2)]
                rstd = sm[:, 34 + (r % 2):35 + (r % 2)]
                act(junkb[:], x_[:], AF.Square, accum=ss)
                tsc('dve', rstd, ss, 1.0 / 1024, 1e-6, ALU.mult, ALU.add)
                act(rstd, rstd, AF.Sqrt)
                P.add('dve', lambda e, rstd=rstd: e.reciprocal(rstd, rstd), reads=[rstd], writes=[rstd])
                tsc('dve', xn[:], x_[:], rstd, None, ALU.mult)
                yield
                pT = psb(0, [128, 8, 128])
                for k in range(8):
                    tr(pT[:, k, :], xn[:, k * 128:(k + 1) * 128], identb[:])
                yield
                cp('act', uT[:, :, rt * 128:(rt + 1) * 128], pT)
                yield

        def genM(p):
            isq = (p % 4 == 3)
            qi = p // 4
            uT = uTB[p % 2]
            qkT = qkTB[p % 2]
            Vp = VpB[p % 2]
            gt = gtB[p % 2]
            Gm = GmB[p % 2]
            def v3(c0):
                return sm[:, c0:c0 + 16].rearrange("p (a b) -> p a b", b=4)
            e1, nlf, t1, t2, ebv, ekv, ekL, dec = [v3(128 + 16 * i_) for i_ in range(8)]
            li = gt[:, :, 0:4]
            zf = gt[:, :, 4:8]
            act(e1, zf, AF.Exp, scale=-1.0)
            act(nlf, e1, AF.Ln, bias=1.0)
            yield
            pg = pss[3]
            nlf2 = sm[:, 128 + 16:128 + 32]
            mm(pg[:, 0:16], [(triF, nlf2)])
            mm(pg[:, 16:32], [(onesF, nlf2)])
            yield
            pgb = pg[:, 0:16].rearrange("p (a b) -> p a b", b=4)
            pgl = pg[:, 16:32].rearrange("p (a b) -> p a b", b=4)
            act(ebv, pgb, AF.Exp, scale=-1.0)
            stt(t1, li, LN_DK, pgb, ALU.add, ALU.add)
            act(ekv, t1, AF.Exp)
            tt('dve', t2, t1, pgl, ALU.subtract)
            act(ekL, t2, AF.Exp)
            act(dec, pgl, AF.Exp, scale=-1.0)
            yield
            for rt in range(4):
                q = qi * 4 + rt
                cols = slice(rt * 128, (rt + 1) * 128)
                if isq:
                    pS2 = pss[7]
                    for h in range(4):
                        mm(pS2[:, h * 128:(h + 1) * 128], [(qkT[:, 4 + h, cols], qkT[:, h, cols])])
                    yield
                    for h in range(4):
                        stt(ATB[h][:], pS2[:, h * 128:(h + 1) * 128], ekv[:, rt, h:h + 1], c01b[:], ALU.mult, ALU.mult)
                    yield
                    for h in range(4):
                        qT_h = qkT[:, h, cols]
                        pN = pss[6]

                        def fnN(e, h=h, rt=rt, qT_h=qT_h, pN=pN, Vp=Vp):
                            e.matmul(pN[:, 0:257], lhsT=ATB[h][:], rhs=Vp[:, rt, h, :], start=True, stop=False)
                            return e.matmul(pN[:, 0:257], lhsT=qT_h, rhs=Cb[:, h, :], start=False, stop=True)
                        P.add('pe', fnN, reads=[ATB[h][:], Vp[:, rt, h, :], qT_h, Cb[:, h, :]], writes=[pN[:, 0:257]])
                        yield
                        tt('dve', d1[:, h:h + 1], pN[:, 256:257], ebv[:, rt, h:h + 1], ALU.mult)
                        stt(d1[:, h:h + 1], d1[:, h:h + 1], -1.0, d1[:, h:h + 1], ALU.mult, ALU.max)
                        tsc('dve', d1[:, h:h + 1], d1[:, h:h + 1], 1.0, None, ALU.max)
                        P.add('dve', lambda e, h=h: e.reciprocal(d1[:, h:h + 1], d1[:, h:h + 1]), reads=[d1[:, h:h + 1]], writes=[d1[:, h:h + 1]])
                        tt('dve', scv[:, h:h + 1], d1[:, h:h + 1], ebv[:, rt, h:h + 1], ALU.mult)
                        tsc('dve', hh[:, h, :], pN[:, 0:256], scv[:, h:h + 1], None, ALU.mult)
                        act(junkb[:, 0:256], hh[:, h, :], AF.Square, accum=hs[:, h:h + 1])
                        yield
                pKv = pss[3][:].bitcast(BF16)[:, 512:1024].rearrange("p (a b) -> p a b", b=128)
                for h in range(4):
                    tr(pKv[:, h, :], qkT[:, 4 + h, cols], identb[:])
                yield
                for h in range(4):
                    tsc('dve', KhatB[h][:], pKv[:, h, :], ekL[:, rt, h:h + 1], None, ALU.mult)
                yield
                for h in range(4):
                    pC = pss[4 + (h % 2)]
                    mm(pC[:, 0:257], [(KhatB[h][:], Vp[:, rt, h, :])])
                    if h % 2 == 1:
                        yield
                        for h2 in (h - 1, h):
                            pC2 = pss[4 + (h2 % 2)]
                            stt(Cst[:, h2, :], Cst[:, h2, :], dec[:, rt, h2:h2 + 1], pC2[:, 0:257], ALU.mult, ALU.add)
                            cp('pool', Cb[:, h2, :], Cst[:, h2, :])
                        yield
                if isq:
                    tsc('dve', rn, hs, 1.0 / 256, 1e-6, ALU.mult, ALU.add)
                    act(rn, rn, AF.Sqrt)
                    P.add('dve', lambda e: e.reciprocal(rn, rn), reads=[rn], writes=[rn])
                    for h in range(4):
                        stt(hh[:, h, :], hh[:, h, :], rn[:, h:h + 1], mnb[:, h * 256:(h + 1) * 256], ALU.mult, ALU.mult)
                    tt('pool', hmg[:], hh[:].rearrange("p a b -> p (a b)"), Gm[:, rt, :], ALU.mult)
                    dma(HM_s[q], hmg[:])
                    yield

        def interleave(gens):
            while gens:
                gens.sort(key=lambda x: x[1] / x[2])
                gq = gens[0]
                try:
                    next(gq[0])
                    gq[1] += 1
                except StopIteration:
                    gens.remove(gq)

        for _ in genN(0):
            pass
        for step in range(NPOS + 1):
            gens = []
            if step + 1 < NPOS:
                gens.append([genN(step + 1), 0, 12])
            if step < NPOS:
                gens.append([genP(step), 0, 8 + 4 + 1 + (8 if step % 4 == 3 else 0) + 8 + 4 + 1 + (21 if step % 4 == 3 else 0)])
            if step == 1:
                gens.append([gen0_rest(), 0, 8])
            if step >= 1:
                pm_ = step - 1
                gens.append([genM(pm_), 0, int(1.4 * (3 + 4 * (6 + (11 if pm_ % 4 == 3 else 0))))])
            interleave(gens)

        bump[0] = 0
        scoresB = [carve([128, 8192], F32) for _ in range(2)]
        sel = carve([128, 8192], BF16)
        selTB = [carve([128, 64, 128], BF16) for _ in range(2)]
        rl = [carve([128, 512], F32) for _ in range(2)]
        kc = [carve([128, 4, 512], BF16) for _ in range(2)]
        vc = [carve([128, 4, 520], BF16) for _ in range(2)]
        Pm = [carve([128, 2, 256], BF16) for _ in range(3)]
        Gt = carve([128, 256, 8], F32)
        vmb = carve([128, 1536], F32)
        GtB = carve([128, 2, 8, 128], BF16)
        aqtB = [carve([128, 4, 256], BF16) for _ in range(2)]
        iqtB = [carve([128, 8, 128], BF16) for _ in range(2)]
        gatB = [carve([128, 512], BF16) for _ in range(2)]
        hatB = [carve([128, 512], BF16) for _ in range(2)]
        wqtB = [carve([128, 8], F32) for _ in range(2)]
        print("phase B arena bytes", bump[0])
        dma(vmb[:], vmask.partition_broadcast(128))
        for i_ in range(2):
            memset('pool', iqtB[i_][:], 0.0)
            memset('pool', aqtB[i_][:], 0.0)
        for s_ in range(128):
            dma(Gt[s_:s_ + 1, :, :], FV_s[128 - s_:128 - s_ + 256, :].rearrange("(o a) h -> o a h", o=1))
        for j_ in range(2):
            for h_ in range(8):
                cp('pool', GtB[:, j_, h_, :], Gt[:, j_ * 128:(j_ + 1) * 128, h_])
        cnt_i = [0]
        cnt_a = [0]
        kvi = [0]

        def geom(q):
            g = q // 4
            r4 = q % 4
            R = 4 * (4 * g + 3) + r4
            return R, (R + 1) * 128, (R + 4) // 4

        def skewed(unit_stages, nst):
            n = len(unit_stages)
            for step in range(n + nst - 1):
                for st_ in range(nst - 1, -1, -1):
                    u = step - st_
                    if 0 <= u < n:
                        unit_stages[u][st_]()
                yield

        def gen_idx(q):
            R, NK, nch = geom(q)
            scores = scoresB[q % 2]
            iqt = iqtB[q % 2]
            wqt = wqtB[q % 2]
            for hp in range(2):
                dma(iqt[hp * 64:(hp + 1) * 64, hp::2, :], IQ_s[hp * 64:(hp + 1) * 64, :, q * 128:(q + 1) * 128])
            dma(wqt[:], WQ_s[q])
            us = []
            for h in range(8):
                pr = h // 2
                ro = (h % 2) * 64
                for c in range(nch):
                    n = min(512, NK - c * 512)
                    ui = cnt_i[0]
                    cnt_i[0] += 1
                    pb = pss[1 + ui % 2]
                    rr_ = rl[ui % 2]
                    sc = scores[:, c * 512:c * 512 + n]

                    def s0(pb=pb, n=n, h=h, c=c):
                        mm(pb[:, 0:n], [(iqt[:, h, :], ikT[:, c * 512:c * 512 + n])])

                    def s1(pb=pb, n=n, rr_=rr_):
                        act(rr_[:, 0:n], pb[:, 0:n], AF.Relu)

                    def s2(rr_=rr_, n=n, sc=sc, h=h):
                        if h == 0:
                            tsc('dve', sc, rr_[:, 0:n], wqt[:, 0:1], None, ALU.mult)
                        else:
                            stt(sc, rr_[:, 0:n], wqt[:, h:h + 1], sc, ALU.mult, ALU.add)
                    us.append((s0, s1, s2))
            for _ in skewed(us, 3):
                yield
            tt('dve', scores[:, R * 128:(R + 1) * 128], scores[:, R * 128:(R + 1) * 128], cnegF, ALU.add)
            tt('pool', scores[:, 0:1536], scores[:, 0:1536], vmb[:], ALU.add)
            yield

        def gen_bis(q):
            R, NK, nch = geom(q)
            scores = scoresB[q % 2]
            selT = selTB[q % 2]
            lo = sm[:, 96 + 8 * (q % 2):97 + 8 * (q % 2)]
            mid = sm[:, 97 + 8 * (q % 2):98 + 8 * (q % 2)]
            cnt = sm[:, 98 + 8 * (q % 2):99 + 8 * (q % 2)]
            stp = sm[:, 99 + 8 * (q % 2):100 + 8 * (q % 2)]
            nmid = sm[:, 100 + 8 * (q % 2):101 + 8 * (q % 2)]
            sA = sm[:, 101 + 8 * (q % 2):102 + 8 * (q % 2)]
            Tt = sm[:, 102 + 8 * (q % 2):103 + 8 * (q % 2)]
            NA = int(round(ACT_FRAC * NK / 128.0)) * 128
            memset('dve', mid, 0.0)
            for it in range(NBIS):
                hw = 2048.0 / (2 ** (it + 1))
                hw_next = hw / 2.0 if it + 1 < NBIS else hw
                act(sel[:, 0:NA], scores[:, 0:NA], AF.Sign, bias=mid, scale=-1.0, accum=sA)
                P.add('dve', lambda e, NK=NK, NA=NA, scores=scores, mid=mid, cnt=cnt: e.tensor_scalar(sel[:, NA:NK], scores[:, NA:NK], mid, None, ALU.is_ge, ALU.add, accum_out=cnt),
                      reads=[scores[:, NA:NK], mid], writes=[sel[:, NA:NK], cnt])
                stt(Tt, cnt, 2.0, sA, ALU.mult, ALU.subtract)
                tsc('dve', stp, Tt, 511.0 - NA, hw, ALU.is_ge, ALU.mult)
                stt(mid, stp, -hw_next, mid, ALU.add, ALU.add)
                yield
            lo = mid
            tsc('dve', sel[:, 0:NK], scores[:, 0:NK], lo, None, ALU.is_lt)
            yield
            for kb0 in range(0, R + 1, 8):
                nb_ = min(8, R + 1 - kb0)
                pT = psb(0, [128, 8, 128])
                for i_ in range(nb_):
                    tr(pT[:, i_, :], sel[:, (kb0 + i_) * 128:(kb0 + i_ + 1) * 128], identb[:])
                cp('act', selT[:, kb0:kb0 + nb_, :], pT[:, 0:nb_, :])
                yield

        def gen_att(q):
            R, NK, nch = geom(q)
            selT = selTB[q % 2]
            aqt = aqtB[q % 2]
            gat = gatB[q % 2]
            hat = hatB[q % 2]
            rd8 = sm[:, 112 + 8 * (q % 2):120 + 8 * (q % 2)]
            dma(aqt[0:64, :, 0:128], AQ_s[0:64, :, q * 128:(q + 1) * 128])
            dma(aqt[64:128, :, 128:256], AQ_s[64:128, :, q * 128:(q + 1) * 128])
            dma(gat[:], GA_s[q])
            pO = [pss[5], pss[6]]
            us = []
            for c in range(nch):
                nb_ = min(4, R + 1 - 4 * c)
                kc_ = kc[kvi[0] % 2]
                vc_ = vc[kvi[0] % 2]
                kvi[0] += 1
                for pr in range(4):
                    for bh in range((nb_ + 1) // 2):
                        blks = [b_ for b_ in (2 * bh, 2 * bh + 1) if b_ < nb_]
                        ui = cnt_a[0]
                        cnt_a[0] += 1
                        pS = (pss[3], pss[4])[ui % 2]
                        pm = Pm[ui % 3]

                        def s0(c=c, pr=pr, bh=bh, blks=blks, nb_=nb_, kc_=kc_, vc_=vc_, pS=pS):
                            if pr == 0 and bh == 0:
                                dma(kc_[:, :, 0:nb_ * 128], KT_s[:, :, c * 512:c * 512 + nb_ * 128].rearrange("a d s -> d a s"))
                                dma(vc_[:, 0:nb_, :], VA_s[4 * c:4 * c + nb_].rearrange("r t c -> t r c"))
                            for i_, b_ in enumerate(blks):
                                def fnQ(e, i_=i_, b_=b_):
                                    kb = 4 * c + b_
                                    e.matmul(pS[:, i_ * 256:(i_ + 1) * 256], lhsT=kc_[:, pr, b_ * 128:(b_ + 1) * 128], rhs=aqt[:, pr, :], start=True, stop=False)
                                    if kb == R or kb == R - 1:
                                        j_ = 0 if kb == R else 1
                                        e.matmul(pS[:, i_ * 256:(i_ + 1) * 256], lhsT=identb[:], rhs=GtB[:, j_, 2 * pr:2 * pr + 2, :], start=False, stop=False)
                                    e.matmul(pS[:, i_ * 256:i_ * 256 + 128], lhsT=negI[:], rhs=selT[:, 4 * c + b_, :], start=False, stop=False)
                                    return e.matmul(pS[:, i_ * 256 + 128:(i_ + 1) * 256], lhsT=negI[:], rhs=selT[:, 4 * c + b_, :], start=False, stop=True)
                                P.add('pe', fnQ, reads=[kc_[:, pr, b_ * 128:(b_ + 1) * 128], aqt[:, pr, :], selT[:, 4 * c + b_, :], negI[:], identb[:], GtB[:]],
                                      writes=[pS[:, i_ * 256:(i_ + 1) * 256]])

                        def s1(c=c, pr=pr, blks=blks, pS=pS, pm=pm):
                            n_ = len(blks)
                            act(pm[:, 0:n_, :], pS[:, 0:n_ * 256].rearrange("p (a b) -> p a b", b=256), AF.Exp)

                        def s3(c=c, pr=pr, blks=blks, nb_=nb_, pm=pm, vc_=vc_):
                            for hp in range(2):
                                hh_ = 2 * pr + hp
                                po = pO[hh_ // 4]
                                oc = (hh_ % 4) * 65

                                def fnO(e, hp=hp, hh_=hh_, po=po, oc=oc):
                                    ins = None
                                    for i_, b_ in enumerate(blks):
                                        ins = e.matmul(po[:, oc:oc + 65], lhsT=pm[:, i_, hp * 128:(hp + 1) * 128], rhs=vc_[:, b_, hh_ * 65:(hh_ + 1) * 65],
                                                       start=(c == 0 and b_ == 0 and hh_ % 4 == 0), stop=(c == nch - 1 and b_ == nb_ - 1),
                                                       skip_group_check=True)
                                    return ins
                                P.add('pe', fnO, reads=[pm[:, 0:len(blks), hp * 128:(hp + 1) * 128], vc_[:, blks[0]:blks[-1] + 1, hh_ * 65:(hh_ + 1) * 65]], writes=[po[:, oc:oc + 65]])
                        us.append((s0, s1, s3))
            for _ in skewed(us, 3):
                yield
            for hf in range(2):
                po = pO[hf]
                P.add('dve', lambda e, po=po, hf=hf, rd8=rd8: e.reciprocal(rd8[:, hf * 4:hf * 4 + 4], po[:, 64:260:65]),
                      reads=[po[:, 0:260]], writes=[rd8[:, hf * 4:hf * 4 + 4]])
                for h4 in range(4):
                    h = hf * 4 + h4
                    stt(hat[:, h * 64:(h + 1) * 64], po[:, h4 * 65:h4 * 65 + 64], rd8[:, h:h + 1], gat[:, h * 64:(h + 1) * 64], ALU.mult, ALU.mult)
            dma(HA_s[q], hat[:])
            yield

        def units(kind, q):
            R, NK, nch = geom(q)
            if kind == 0:
                return 8 * nch + 3
            if kind == 1:
                return NBIS + 1 + (R + 8) // 8
            return 8 * nch + 4

        for step in range(NQT + 2):
            gens = []
            for kind, qq in ((0, step), (1, step - 1), (2, step - 2)):
                if 0 <= qq < min(NQT, QLIM):
                    gf = (gen_idx, gen_bis, gen_att)[kind]
                    gens.append([gf(qq), 0, units(kind, qq)])
            while gens:
                gens.sort(key=lambda x: x[1] / x[2])
                gq = gens[0]
                try:
                    next(gq[0])
                    gq[1] += 1
                except StopIteration:
                    gens.remove(gq)

        bump[0] = 0
        wo = carve([128, 12, 1024], BF16)
        wg = carve([128, 8, 1024], BF16)
        wp = carve([128, 2, 1024], BF16)
        stg = [carve([128, 1024], F32) for _ in range(2)]
        nfb = carve([128, 1024], F32)
        CB = []
        for i_ in range(3):
            CB.append(dict(
                cat=carve([128, 1536], BF16), catT=carve([128, 12, 128], BF16), xres=carve([128, 1024], F32),
                h1=carve([128, 1024], F32), h1b=carve([128, 1024], BF16), h1T=carve([128, 8, 128], BF16),
                pt=carve([128, 256], F32), ptb=carve([128, 256], BF16), ptT=carve([128, 2, 128], BF16),
                sg=carve([128, 1024], F32), ot=carve([128, 1024], F32), junk=carve([128, 1024], BF16)))
        print("phase C arena bytes", bump[0])
        dma(nfb[:], norm_final.partition_broadcast(128))
        si = [0]

        def load_w(dst, src_row0):
            s_ = stg[si[0] % 2]
            si[0] += 1
            dma(s_[:], src_row0)
            cp('dve' if si[0] % 2 else 'pool', dst, s_[:])
        for k in range(12):
            load_w(wo[:, k, :], w_out[k * 128:(k + 1) * 128, :])
        for k in range(8):
            load_w(wg[:, k, :], w_gate[k * 128:(k + 1) * 128, :])
        for k in range(2):
            load_w(wp[:, k, :], w_ple[k * 128:(k + 1) * 128, :])
        cbank = [0]

        def cbk():
            cbank[0] += 1
            return pss[(1, 2, 4, 5)[cbank[0] % 4]]

        def genC(q):
            B_ = CB[q % 3]
            cat, catT, xres, h1, h1b, h1T = B_['cat'], B_['catT'], B_['xres'], B_['h1'], B_['h1b'], B_['h1T']
            pt_, ptb, ptT, sg, ot, junkc = B_['pt'], B_['ptb'], B_['ptT'], B_['sg'], B_['ot'], B_['junk']
            ss2 = sm[:, 120 + 2 * (q % 3):121 + 2 * (q % 3)]
            rs2 = sm[:, 121 + 2 * (q % 3):122 + 2 * (q % 3)]
            pbT = (0, 3)[q % 2]
            pbT2 = (7, 6)[q % 2]
            g = q // 4
            r4 = q % 4
            r = 4 * (4 * g + 3) + r4
            dma(cat[:, 0:1024], HM_s[q])
            dma(cat[:, 1024:1536], HA_s[q])
            dma(xres[:], xk[r * 128:(r + 1) * 128, :])
            dma(pt_[:], pq[q * 128:(q + 1) * 128, :])
            yield
            for k0 in (0, 8):
                nb_ = 8 if k0 == 0 else 4
                pT = psb(pbT, [128, 8, 128])
                for i_ in range(nb_):
                    tr(pT[:, i_, :], cat[:, (k0 + i_) * 128:(k0 + i_ + 1) * 128], identb[:])
                yield
                cp('act', catT[:, k0:k0 + nb_, :], pT[:, 0:nb_, :])
                yield
            cp('pool', ptb[:], pt_[:])
            for half in range(2):
                pb = cbk()
                mm(pb[:], [(catT[:, k, :], wo[:, k, half * 512:(half + 1) * 512]) for k in range(12)])
                yield
                tt('dve', h1[:, half * 512:(half + 1) * 512], pb[:], xres[:, half * 512:(half + 1) * 512], ALU.add)
                yield
            cp('pool', h1b[:], h1[:])
            yield
            pT = psb(pbT, [128, 8, 128])
            for k in range(8):
                tr(pT[:, k, :], h1b[:, k * 128:(k + 1) * 128], identb[:])
            pT2 = psb(pbT2, [128, 2, 128])
            for k in range(2):
                tr(pT2[:, k, :], ptb[:, k * 128:(k + 1) * 128], identb[:])
            yield
            cp('act', h1T[:], pT)
            cp('act', ptT[:], pT2)
            yield
            for half in range(2):
                hs_ = slice(half * 512, (half + 1) * 512)
                pb = cbk()
                mm(pb[:], [(h1T[:, k, :], wg[:, k, hs_]) for k in range(8)])
                pb2 = cbk()
                mm(pb2[:], [(ptT[:, k, :], wp[:, k, hs_]) for k in range(2)])
                yield
                act(sg[:, hs_], pb[:], AF.Sigmoid)
                yield
                tt('dve', sg[:, hs_], pb2[:], sg[:, hs_], ALU.mult)
                yield
            tt('pool', h1[:], h1[:], sg[:], ALU.add)
            yield
            act(junkc[:], h1[:], AF.Square, accum=ss2)
            yield
            tsc('dve', rs2, ss2, 1.0 / 1024, 1e-6, ALU.mult, ALU.add)
            act(rs2, rs2, AF.Sqrt)
            P.add('dve', lambda e, rs2=rs2: e.reciprocal(rs2, rs2), reads=[rs2], writes=[rs2])
            yield
            stt(ot[:], h1[:], rs2, nfb[:], ALU.mult, ALU.mult)
            dma(out[q * 128:(q + 1) * 128, :], ot[:])
            yield

        def roll(gen_list, lag):
            active = []
            nxt = 0
            started_at = []
            while nxt < len(gen_list) or active:
                if nxt < len(gen_list) and (not active or active[-1][1] >= lag):
                    active.append([gen_list[nxt], 0])
                    nxt += 1
                for a_ in list(active):
                    try:
                        next(a_[0])
                        a_[1] += 1
                    except StopIteration:
                        active.remove(a_)
        roll([genC(q) for q in range(min(NQT, QLIM))], 8)

        P.prepare()
        print("ops:", {e: len(P.ops[e]) for e in Prog.ENGS})
        with nc.Block() as block:
            @block.sync
            def _(e):
                P.run('sp', e, sems, dsems)
                P.final_waits(e, dsems)

            @block.scalar
            def _(e):
                P.run('act', e, sems, dsems)

            @block.vector
            def _(e):
                P.run('dve', e, sems, dsems)

            @block.gpsimd
            def _(e):
                P.run('pool', e, sems, dsems)

            @block.tensor
            def _(e):
                P.run('pe', e, sems, dsems)
    return nc


def make_core_inputs(inputs):
    x = np.asarray(inputs["x"], np.float32)
    p = np.asarray(inputs["p"], np.float32)[0]
    consts = make_consts()
    shared = {
        "w_in": np.ascontiguousarray(np.asarray(inputs["w_in"], np.float32)[0]),
        "b_gate": np.ascontiguousarray(np.asarray(inputs["b_gate"], np.float32)[0:1]),
        "conv_w": np.ascontiguousarray(np.asarray(inputs["conv_w"], np.float32)[0].reshape(4, 8, 128).transpose(2, 0, 1).reshape(128, 32)),
        "w_out": np.ascontiguousarray(np.asarray(inputs["w_out"], np.float32)[0]),
        "norm_in": np.ascontiguousarray(np.asarray(inputs["norm_in"], np.float32)[0].reshape(8, 128).T),
        "m_norm": np.ascontiguousarray(np.asarray(inputs["m_norm"], np.float32)[0:1]),
        "rel_bias": np.ascontiguousarray(np.asarray(inputs["rel_bias"], np.float32)),
        "w_ple": np.ascontiguousarray(np.asarray(inputs["w_ple"], np.float32)[0]),
        "w_ple_gate": np.ascontiguousarray(np.asarray(inputs["w_ple_gate"], np.float32)[0]),
        "norm_final": np.ascontiguousarray(np.asarray(inputs["norm_final"], np.float32).reshape(1, D)),
        "consts": consts,
    }
    maps = []
    for c in range(8):
        b, j = c // 4, c % 4
        xkc = np.zeros((8192, D), np.float32)
        n0 = 512 * (3 - j)
        xkc[n0:] = x[b, 0:8192 - n0]
        pqc = np.zeros((2048, 256), np.float32)
        for g in range(4):
            ch = 4 * g + j
            pqc[g * 512:(g + 1) * 512] = p[b, ch * 512:(ch + 1) * 512]
        vm = np.zeros((1, 1536), np.float32)
        vm[0, 0:n0] = -BIG
        m = dict(shared)
        m["xk"] = xkc
        m["pq"] = pqc
        m["vmask"] = vm
        maps.append(m)
    return maps


_NC_CACHE = {}


def kernel(**inputs):
    maps = make_core_inputs(inputs)
    if 'nc' not in _NC_CACHE:
        _NC_CACHE['nc'] = build_program(DEBUG)
    nc = _NC_CACHE['nc']
    res = run_bass_kernel_spmd(nc, maps, core_ids=list(range(8)))
    outp = np.zeros((2, 8192, D), np.float32)
    for c in range(8):
        b, j = c // 4, c % 4
        o = np.asarray(res.results[c]["out"], np.float32)
        for g in range(4):
            ch = 4 * g + j
            outp[b, ch * 512:(ch + 1) * 512] = o[g * 512:(g + 1) * 512]
    if DEBUG:
        kernel.last = res
    return outp
```

```python
import math
QLIM = 16
from contextlib import ExitStack
import numpy as np
import concourse.bass as bass
import concourse.mybir as mybir
from concourse.bass_utils import run_bass_kernel_spmd

F32 = mybir.dt.float32
BF16 = mybir.dt.bfloat16
ALU = mybir.AluOpType
AF = mybir.ActivationFunctionType

DEBUG = False
NBIS = 34
ACT_FRAC = 0.54
BIG = 1.0e9


def _dsize(dt):
    if dt == F32:
        return 4
    if dt == BF16:
        return 2
    s = str(dt)
    if '32' in s:
        return 4
    if '16' in s:
        return 2
    if '64' in s:
        return 8
    return 1


class Op:
    __slots__ = ('eng', 'fn', 'deps', 'sig', 'cnt', 'dma', 'slot', 'dcnt', 'idx')


class Prog:
    ENGS = ['sp', 'act', 'dve', 'pool', 'pe']
    NSLOT = 16
    SAME_SYNC = True

    def __init__(self, nc):
        self.nc = nc
        self.ops = {e: [] for e in self.ENGS}
        self.track = {}
        self.slot_last = {}
        self.rr = {e: 0 for e in self.ENGS}
        self.nops = 0

    def region(self, ap):
        t = ap.tensor
        sp = str(ap.space)
        dims = ap.ap
        es = _dsize(ap.dtype)
        off = int(ap.offset)
        if 'SB' in sp or 'PSUM' in sp:
            pstep = dims[0][0]
            pcnt = dims[0][1]
            if pstep == 0:
                plo = 0
                foff = off
            else:
                plo = off // pstep
                foff = off - plo * pstep
            lo = foff
            hi = foff
            for st, c in dims[1:]:
                if st >= 0:
                    hi += st * (c - 1)
                else:
                    lo += st * (c - 1)
            if 'PSUM' in sp:
                return ((sp, t.name), 0, 128, 0, 2048)
            return ((sp, t.name), plo, plo + pcnt, lo * es, (hi + 1) * es)
        lo = off
        hi = off
        for st, c in dims:
            if st >= 0:
                hi += st * (c - 1)
            else:
                lo += st * (c - 1)
        return ((sp, t.name), 0, 1, lo * es, (hi + 1) * es)

    def add(self, eng, fn, reads=(), writes=(), dma=False):
        op = Op()
        op.eng = eng
        op.fn = fn
        op.sig = False
        op.cnt = 0
        op.dma = dma
        op.slot = None
        op.dcnt = 0
        op.idx = self.nops
        self.nops += 1
        deps = set()
        rregs = [self.region(a) for a in reads]
        wregs = [self.region(a) for a in writes]
        for (key, plo, phi, lo, hi) in rregs:
            for e in self.track.get(key, ()):
                if e[5] and e[0] < phi and plo < e[1] and e[2] < hi and lo < e[3]:
                    deps.add(e[4])
        for (key, plo, phi, lo, hi) in wregs:
            for e in self.track.get(key, ()):
                if e[0] < phi and plo < e[1] and e[2] < hi and lo < e[3]:
                    deps.add(e[4])
        for (key, plo, phi, lo, hi) in wregs:
            lst = self.track.setdefault(key, [])
            lst[:] = [e for e in lst if not (plo <= e[0] and e[1] <= phi and lo <= e[2] and e[3] <= hi)]
            lst.append((plo, phi, lo, hi, op, True))
        for (key, plo, phi, lo, hi) in rregs:
            self.track.setdefault(key, []).append((plo, phi, lo, hi, op, False))
        if dma:
            s = self.rr[eng] % self.NSLOT
            self.rr[eng] += 1
            prev = self.slot_last.get((eng, s))
            op.slot = s
            op.dcnt = (prev.dcnt if prev is not None else 0) + 1
            if prev is not None:
                deps.add(prev)
            self.slot_last[(eng, s)] = op
        deps.discard(op)
        op.deps = list(deps)
        self.ops[eng].append(op)
        return op

    def _skip(self, d, ename):
        return d.eng == ename and (ename in ('pe', 'sp') or not self.SAME_SYNC)

    def prepare(self):
        for e in self.ENGS:
            for op in self.ops[e]:
                for d in op.deps:
                    if d.dma or self._skip(d, op.eng):
                        continue
                    d.sig = True
        for e in self.ENGS:
            c = 0
            for op in self.ops[e]:
                if op.sig:
                    c += 1
                    op.cnt = c

    def run(self, ename, eng, sems, dsems):
        waited = {}
        for op in self.ops[ename]:
            for d in sorted(op.deps, key=lambda o: o.idx):
                if d.dma:
                    key = ('d', d.eng, d.slot)
                    val = 16 * d.dcnt
                    sem = dsems[d.eng][d.slot]
                else:
                    if self._skip(d, ename):
                        continue
                    key = ('e', d.eng)
                    val = d.cnt
                    sem = sems[d.eng]
                if waited.get(key, 0) >= val:
                    continue
                waited[key] = val
                eng.wait_ge(sem, val)
            ins = op.fn(eng)
            if op.dma:
                ins.then_inc(dsems[ename][op.slot], 16)
            elif op.sig:
                ins.then_inc(sems[ename], 1)

    def final_waits(self, eng, dsems):
        for (e, s), op in self.slot_last.items():
            eng.wait_ge(dsems[e][s], 16 * op.dcnt)


D = 1024
NPOS = 16
NRT = 64
NQT = 16
PT = 6736
C_MQ, C_MK, C_MV, C_MO, C_MZ, C_IF, C_AQ, C_AK, C_AV, C_AZ, C_IQ, C_IK, C_IW = (
    0, 512, 1024, 2048, 3072, 4096, 4104, 4616, 5128, 5640, 6152, 6664, 6728)
LN_DK = math.log(128.0 ** -0.5)
SLABS = [(0, 512), (512, 512), (1024, 512), (1536, 512), (2048, 512), (2560, 512), (3072, 512), (3584, 512),
         (4096, 8), (4104, 512), (4616, 512), (5128, 512), (5640, 512), (6152, 512), (6664, 64), (6728, 8)]
(S_QK0, S_QK1, S_MV0, S_MV1, S_MO0, S_MO1, S_MZ0, S_MZ1, S_IF, S_AQ, S_AK, S_AV, S_AZ, S_IQ, S_IK, S_IW) = range(16)


def t5_bucket_np(dist):
    max_exact = 16
    d = np.maximum(dist, 1).astype(np.float32)
    large = max_exact + (np.log(d / max_exact) / math.log(128 / max_exact) * (32 - max_exact)).astype(np.int32)
    large = np.minimum(large, 31)
    return np.where(dist < max_exact, dist, large)


def make_consts():
    c = {}
    c['ident'] = np.eye(128, dtype=np.float32)
    s = np.arange(128)[:, None]
    t = np.arange(128)[None, :]
    c['tri'] = (s <= t).astype(np.float32)
    c['ones'] = np.ones((128, 128), np.float32)
    c['cneg'] = np.where(t <= s, 0.0, -BIG).astype(np.float32)
    oh = np.zeros((128, 384), np.float32)
    d = np.arange(384) - 128
    bk = t5_bucket_np(np.maximum(d, 0))
    for i in range(384):
        if d[i] >= 0:
            oh[bk[i], i] = 1.0
    c['oh'] = oh
    return np.concatenate([c['ident'], c['tri'], c['ones'], c['cneg'], c['oh']], axis=1).astype(np.float32)


CO_ID, CO_TRI, CO_ONES, CO_CNEG, CO_OH, CO_END = 0, 128, 256, 384, 512, 896


def build_program(debug=False):
    nc = bass.Bass("TRN2", target_bir_lowering=False)
    EI = "ExternalInput"
    xk = nc.dram_tensor("xk", [8192, D], F32, kind=EI).ap()
    pq = nc.dram_tensor("pq", [2048, 256], F32, kind=EI).ap()
    vmask = nc.dram_tensor("vmask", [1, 1536], F32, kind=EI).ap()
    w_in = nc.dram_tensor("w_in", [D, PT], F32, kind=EI).ap()
    b_gate = nc.dram_tensor("b_gate", [1, 8], F32, kind=EI).ap()
    conv_w = nc.dram_tensor("conv_w", [128, 32], F32, kind=EI).ap()
    w_out = nc.dram_tensor("w_out", [1536, D], F32, kind=EI).ap()
    norm_in = nc.dram_tensor("norm_in", [128, 8], F32, kind=EI).ap()
    m_norm = nc.dram_tensor("m_norm", [1, D], F32, kind=EI).ap()
    rel_bias = nc.dram_tensor("rel_bias", [32, 8], F32, kind=EI).ap()
    w_ple = nc.dram_tensor("w_ple", [256, D], F32, kind=EI).ap()
    w_gate = nc.dram_tensor("w_ple_gate", [D, D], F32, kind=EI).ap()
    norm_final = nc.dram_tensor("norm_final", [1, D], F32, kind=EI).ap()
    consts = nc.dram_tensor("consts", [128, CO_END], F32, kind=EI).ap()
    out = nc.dram_tensor("out", [2048, D], F32, kind="ExternalOutput").ap()
    SK = "ExternalOutput" if debug else "Internal"
    W_s = nc.dram_tensor("W_s", [len(SLABS), 128, 8, 512], BF16, kind="Internal").ap()
    KT_s = nc.dram_tensor("KT_s", [4, 128, 8192], BF16, kind=SK).ap()
    VA_s = nc.dram_tensor("VA_s", [64, 128, 520], BF16, kind=SK).ap()
    HM_s = nc.dram_tensor("HM_s", [16, 128, 1024], BF16, kind=SK).ap()
    AQ_s = nc.dram_tensor("AQ_s", [128, 4, 2048], BF16, kind=SK).ap()
    IQ_s = nc.dram_tensor("IQ_s", [128, 4, 2048], BF16, kind=SK).ap()
    GA_s = nc.dram_tensor("GA_s", [16, 128, 512], BF16, kind=SK).ap()
    HA_s = nc.dram_tensor("HA_s", [16, 128, 512], BF16, kind=SK).ap()
    WQ_s = nc.dram_tensor("WQ_s", [16, 128, 8], F32, kind=SK).ap()
    FV_s = nc.dram_tensor("FV_s", [384, 8], F32, kind=SK).ap()
    SC_s = nc.dram_tensor("SC_s", [16, 128, 8192], BF16, kind=SK).ap() if debug else None

    es = ExitStack()
    with es:
        def sb(name, shape, dt):
            return es.enter_context(nc.sbuf_tensor(name, shape, dt))

        ARENA = 166 * 1024
        arena = sb("arena", [128, ARENA // 2], BF16)
        bump = [0]

        def carve(shape, dt):
            n = 1
            for s_ in shape[1:]:
                n *= s_
            nb = n * _dsize(dt)
            nb = (nb + 63) // 64 * 64
            o = bump[0]
            bump[0] += nb
            assert bump[0] <= ARENA, (bump[0], ARENA)
            v = arena[0:shape[0], o // 2:(o + nb) // 2]
            if dt == F32:
                v = v.bitcast(F32)
            v = v[:, 0:n]
            if len(shape) == 3:
                v = v.rearrange("p (a b) -> p a b", b=shape[2])
            elif len(shape) == 4:
                v = v.rearrange("p (a b c) -> p a b c", b=shape[2], c=shape[3])
            return v

        cst = sb("cst", [128, CO_END], F32)
        identb = sb("identb", [128, 128], BF16)
        c01b = sb("c01b", [128, 128], BF16)
        negI = sb("negI", [128, 128], BF16)
        ikT = sb("ikT", [128, 8192], BF16)
        g8 = sb("g8", [128, 8], F32)
        cw = sb("cw", [128, 4, 8], F32)
        bgb = sb("bgb", [128, 8], F32)
        relb = sb("relb", [32, 8], F32)
        rel31 = sb("rel31", [32, 8], F32)
        sm = sb("sm", [128, 256], F32)
        pss = [es.enter_context(nc.psum_tensor(f"ps{i}", [128, 512], F32)) for i in range(8)]
        sems = {e: es.enter_context(nc.semaphore("s_" + e)) for e in Prog.ENGS}
        dsems = {'sp': [es.enter_context(nc.semaphore(f"d_sp{i}")) for i in range(Prog.NSLOT)]}
        P = Prog(nc)
        ident = cst[:, CO_ID:CO_ID + 128]
        triF = cst[:, CO_TRI:CO_TRI + 128]
        onesF = cst[:, CO_ONES:CO_ONES + 128]
        cnegF = cst[:, CO_CNEG:CO_CNEG + 128]
        ohF = cst[0:32, CO_OH:CO_OH + 384]

        def dma(o, i):
            P.add('sp', lambda e: e.dma_start(out=o, in_=i), reads=[i], writes=[o], dma=True)

        def psb(i, shape):
            v = pss[i][:].bitcast(BF16)
            n = 1
            for s_ in shape[1:]:
                n *= s_
            v = v[0:shape[0], 0:n]
            if len(shape) == 3:
                v = v.rearrange("p (a b) -> p a b", b=shape[2])
            return v

        def tsc(eng, o, i, s1, s2, op0, op1=None, reads=None):
            rd = [i] + [s for s in (s1, s2) if not isinstance(s, (int, float)) and s is not None]
            if op1 is None:
                P.add(eng, lambda e: e.tensor_scalar(o, i, s1, None, op0), reads=rd, writes=[o])
            else:
                P.add(eng, lambda e: e.tensor_scalar(o, i, s1, s2, op0, op1), reads=rd, writes=[o])

        def tt(eng, o, a, b, op):
            P.add(eng, lambda e: e.tensor_tensor(o, a, b, op), reads=[a, b], writes=[o])

        def stt(o, a, s, b, op0, op1):
            rd = [a, b] + ([] if isinstance(s, (int, float)) else [s])
            P.add('dve', lambda e: e.scalar_tensor_tensor(o, a, s, b, op0, op1), reads=rd, writes=[o])

        def act(o, i, func, scale=None, bias=None, accum=None):
            kw = {}
            rd = [i]
            wr = [o]
            if scale is not None:
                kw['scale'] = scale
                if not isinstance(scale, (int, float)):
                    rd.append(scale)
            if bias is not None:
                kw['bias'] = bias
                if not isinstance(bias, (int, float)):
                    rd.append(bias)
            if accum is not None:
                kw['accum_out'] = accum
                wr.append(accum)
            P.add('act', lambda e: e.activation(out=o, in_=i, func=func, **kw), reads=rd, writes=wr)

        def cp(eng, o, i):
            if eng == 'act':
                P.add('act', lambda e: e.copy(o, i), reads=[i], writes=[o])
            else:
                P.add(eng, lambda e: e.tensor_copy(o, i), reads=[i], writes=[o])

        def mm(o, pairs, reads_extra=()):
            rd = []
            for l, r in pairs:
                rd += [l, r]

            def fn(e):
                ins = None
                n = len(pairs)
                for i_, (l, r) in enumerate(pairs):
                    ins = e.matmul(o, lhsT=l, rhs=r, start=(i_ == 0), stop=(i_ == n - 1))
                return ins
            P.add('pe', fn, reads=rd, writes=[o])

        def tr(o, i, idt):
            P.add('pe', lambda e: e.transpose(o, i, idt), reads=[i, idt], writes=[o])

        def memset(eng, o, v):
            P.add(eng, lambda e: e.memset(o, v), writes=[o])

        dma(cst[:], consts)
        dma(g8[:], norm_in)
        dma(cw[:], conv_w.rearrange("p (j c) -> p j c", c=8))
        dma(bgb[:], b_gate.partition_broadcast(128))
        dma(relb[:], rel_bias)
        dma(rel31[:], rel_bias[31:32, :].partition_broadcast(32))
        cp('dve', identb[:], ident)
        cp('dve', c01b[:], triF)
        tsc('dve', negI[:], ident, -30000.0, None, ALU.mult)
        tt('dve', relb[:], relb[:], rel31[:], ALU.subtract)
        fvt = sm[:, 0:24].rearrange("p (a b) -> p a b", b=8)
        for i3 in range(3):
            mm(pss[7][:, i3 * 8:(i3 + 1) * 8], [(ohF[:, i3 * 128:(i3 + 1) * 128], relb[:])])
        cp('dve', sm[:, 0:24], pss[7][:, 0:24])
        dma(FV_s.rearrange("(a p) h -> p a h", p=128), fvt)

        bump[0] = ARENA - 24 * 1024 - 64
        wst = carve([128, 8, 512], F32)
        wsb = carve([128, 8, 512], BF16)
        w_in_v = w_in.rearrange("(k p) c -> p k c", p=128)

        def conv_slab(sl):
            c0, ncl = SLABS[sl]
            dma(wst[:, :, 0:ncl], w_in_v[:, :, c0:c0 + ncl])
            for k in range(8):
                tsc('dve', wsb[:, k, 0:ncl], wst[:, k, 0:ncl], g8[:, k:k + 1], None, ALU.mult)
            if ncl == 64:
                for k in range(8):
                    cp('pool', wsb[:, k, 64:128], wsb[:, k, 0:64])
            dma(W_s[sl], wsb[:])
        converted = set()

        def ensure_conv(sl):
            if sl not in converted:
                converted.add(sl)
                conv_slab(sl)

        def gen0_rest():
            for sl in (S_AQ, S_IQ, S_MO0, S_MO1, S_MZ0, S_MZ1, S_AZ, S_IW):
                ensure_conv(sl)
                yield

        bump[0] = 0
        xt = [carve([128, 1024], F32) for _ in range(2)]
        junkb = carve([128, 1024], BF16)
        xnB = [carve([128, 1024], BF16) for _ in range(2)]
        uTB = [carve([128, 8, 512], BF16) for _ in range(2)]
        wsl = [carve([128, 8, 512], BF16) for _ in range(3)]
        qkpre = carve([128, 8, 516], F32)
        ctmp = [carve([128, 512], F32) for _ in range(2)]
        qkTB = [carve([128, 8, 512], BF16) for _ in range(2)]
        VpB = [carve([128, 4, 4, 257], BF16) for _ in range(2)]
        kst = carve([128, 4, 512], BF16)
        VAst = carve([128, 4, 8, 65], BF16)
        gtB = [carve([128, 4, 8], F32) for _ in range(2)]
        KhatB = [carve([128, 128], BF16) for _ in range(4)]
        mnb = carve([128, 1024], F32)
        Cst = carve([128, 4, 257], F32)
        Cb = carve([128, 4, 257], BF16)
        qst = carve([128, 4, 512], BF16)
        GmB = [carve([128, 4, 1024], BF16) for _ in range(2)]
        slz = [carve([128, 512], F32) for _ in range(2)]
        hh = carve([128, 4, 256], F32)
        hmg = carve([128, 1024], BF16)
        gast = carve([128, 4, 512], BF16)
        wqst = carve([128, 4, 8], F32)
        ATB = [carve([128, 128], BF16) for _ in range(4)]
        print("phase A arena bytes", bump[0])
        dma(mnb[:], m_norm.partition_broadcast(128))
        memset('pool', Cst[:], 0.0)
        memset('pool', Cb[:], 0.0)
        memset('pool', qkpre[:, :, 0:3], 0.0)
        for i_ in range(2):
            memset('pool', VpB[i_][:], 1.0)
        memset('pool', VAst[:], 1.0)
        hs = sm[:, 72:76]
        rn = sm[:, 76:80]
        d1 = sm[:, 80:84]
        scv = sm[:, 84:88]
        wslot = [0]

        def load_slab(sid):
            ensure_conv(sid)
            w = wsl[wslot[0] % 3]
            wslot[0] += 1
            dma(w[:], W_s[sid])
            return w

        pbank = [0]

        def nextbank():
            pbank[0] += 1
            return pss[1 + pbank[0] % 2]

        def xload(r):
            dma(xt[r % 2][:], xk[r * 128:(r + 1) * 128, :])

        def genP(p):
            isq = (p % 4 == 3)
            qi = p // 4
            uT = uTB[p % 2]
            qkT = qkTB[p % 2]
            Vp = VpB[p % 2]
            gt = gtB[p % 2]
            Gm = GmB[p % 2]
            def j_qk(half):
                def f(w):
                    for c4 in range(4):
                        ct = half * 4 + c4
                        pb = nextbank()
                        mm(pb[:], [(w[:, k, c4 * 128:(c4 + 1) * 128], uT[:, k, :]) for k in range(8)])
                        cp('act', qkpre[:, ct, 3:515], pb[:])
                        tm_ = ctmp[ct % 2]
                        tsc('dve', tm_[:], qkpre[:, ct, 0:512], cw[:, 0, ct:ct + 1], None, ALU.mult)
                        for j in range(1, 4):
                            stt(tm_[:], qkpre[:, ct, j:j + 512], cw[:, j, ct:ct + 1], tm_[:], ALU.mult, ALU.add)
                        act(qkT[:, ct, :], tm_[:], AF.Silu)
                        cp('pool', qkpre[:, ct, 0:3], qkpre[:, ct, 512:515])
                        yield
                return f

            def j_fm4(dst, scale, after):
                def f(w):
                    for pr in range(4):
                        pb = nextbank()
                        mm(pb[:], [(w[:, k, pr * 128:(pr + 1) * 128], uT[:, k, :]) for k in range(8)])
                        if scale is None:
                            cp('act', dst[:, pr, :], pb[:])
                        else:
                            act(dst[:, pr, :], pb[:], AF.Copy, scale=scale)
                        yield
                    after()
                return f

            def j_ik(w):
                pb = nextbank()
                mm(pb[:], [(w[:, k, 0:128], uT[:, k, :]) for k in range(8)])
                cp('act', ikT[:, p * 512:(p + 1) * 512], pb[:])
                yield

            def j_mv(half):
                def f(w):
                    for rt in range(4):
                        pb = nextbank()
                        mm(pb[:], [(uT[:, k, rt * 128:(rt + 1) * 128], w[:, k, :]) for k in range(8)])
                        cp('act', Vp[:, rt, 2 * half:2 * half + 2, 0:256], pb[:].rearrange("p (a b) -> p a b", b=256))
                        yield
                return f

            def j_av(w):
                for rt in range(4):
                    pb = nextbank()
                    mm(pb[:], [(uT[:, k, rt * 128:(rt + 1) * 128], w[:, k, :]) for k in range(8)])
                    cp('act', VAst[:, rt, :, 0:64], pb[:].rearrange("p (a b) -> p a b", b=64))
                    yield
                dma(VA_s[4 * p:4 * p + 4].rearrange("r t c -> t r c"), VAst[:].rearrange("p r a b -> p r (a b)"))

            def j_if(w):
                for rt in range(4):
                    pb = nextbank()
                    mm(pb[:, 0:8], [(uT[:, k, rt * 128:(rt + 1) * 128], w[:, k, 0:8]) for k in range(8)])
                    tt('dve', gt[:, rt, :], pb[:, 0:8], bgb[:], ALU.add)
                yield

            def j_mo(half):
                def f(w):
                    for rt in range(4):
                        pb = nextbank()
                        mm(pb[:], [(uT[:, k, rt * 128:(rt + 1) * 128], w[:, k, :]) for k in range(8)])
                        act(Gm[:, rt, half * 512:(half + 1) * 512], pb[:], AF.Sigmoid)
                        yield
                return f

            def j_mz(half):
                def f(w):
                    for rt in range(4):
                        pb = nextbank()
                        mm(pb[:], [(uT[:, k, rt * 128:(rt + 1) * 128], w[:, k, :]) for k in range(8)])
                        z_ = slz[rt % 2]
                        act(z_[:], pb[:], AF.Silu)
                        tt('pool', Gm[:, rt, half * 512:(half + 1) * 512], Gm[:, rt, half * 512:(half + 1) * 512], z_[:], ALU.mult)
                        yield
                return f

            def j_az(w):
                for rt in range(4):
                    pb = nextbank()
                    mm(pb[:], [(uT[:, k, rt * 128:(rt + 1) * 128], w[:, k, :]) for k in range(8)])
                    act(gast[:, rt, :], pb[:], AF.Silu)
                    yield
                dma(GA_s[4 * qi:4 * qi + 4].rearrange("r t c -> t r c"), gast[:])

            def j_iw(w):
                for rt in range(4):
                    pb = nextbank()
                    mm(pb[:, 0:8], [(uT[:, k, rt * 128:(rt + 1) * 128], w[:, k, 0:8]) for k in range(8)])
                    act(wqst[:, rt, :], pb[:, 0:8], AF.Copy, scale=1.0 / (8.0 * math.sqrt(8.0)))
                dma(WQ_s[4 * qi:4 * qi + 4].rearrange("r t c -> t r c"), wqst[:])
                yield

            jobs = [(S_QK0, j_qk(0)), (S_QK1, j_qk(1)),
                    (S_AK, j_fm4(kst, None, lambda: dma(KT_s[:, :, p * 512:(p + 1) * 512].rearrange("a d s -> d a s"), kst[:]))),
                    (S_IK, j_ik)]
            if isq:
                jobs += [(S_AQ, j_fm4(qst, 0.125, lambda: dma(AQ_s[:, :, qi * 512:(qi + 1) * 512], qst[:]))),
                         (S_IQ, j_fm4(qst, None, lambda: dma(IQ_s[:, :, qi * 512:(qi + 1) * 512], qst[:])))]
            jobs += [(S_MV0, j_mv(0)), (S_MV1, j_mv(1)), (S_AV, j_av), (S_IF, j_if)]
            if isq:
                jobs += [(S_MO0, j_mo(0)), (S_MO1, j_mo(1)), (S_MZ0, j_mz(0)), (S_MZ1, j_mz(1)), (S_AZ, j_az), (S_IW, j_iw)]
            slabs = [load_slab(jobs[0][0]), load_slab(jobs[1][0])]
            for ji, (sid, jf) in enumerate(jobs):
                if ji + 2 < len(jobs):
                    slabs.append(load_slab(jobs[ji + 2][0]))
                for _ in jf(slabs[ji]):
                    yield

        def genN(p):
            uT = uTB[p % 2]
            if p == 0:
                xload(0)
            for rt in range(4):
                r = 4 * p + rt
                if r + 1 < NRT:
                    xload(r + 1)
                x_ = xt[r % 2]
                xn = xnB[r % 2]
                ss = sm[:, 32 + (r % 2):33 + (r % 2)]
                rstd = sm[:, 34 + (r % 2):35 + (r % 2)]
                act(junkb[:], x_[:], AF.Square, accum=ss)
                tsc('dve', rstd, ss, 1.0 / 1024, 1e-6, ALU.mult, ALU.add)
                act(rstd, rstd, AF.Sqrt)
                P.add('dve', lambda e, rstd=rstd: e.reciprocal(rstd, rstd), reads=[rstd], writes=[rstd])
                tsc('dve', xn[:], x_[:], rstd, None, ALU.mult)
                yield
                pT = psb(0, [128, 8, 128])
                for k in range(8):
                    tr(pT[:, k, :], xn[:, k * 128:(k + 1) * 128], identb[:])
                yield
                cp('act', uT[:, :, rt * 128:(rt + 1) * 128], pT)
                yield

        def genM(p):
            isq = (p % 4 == 3)
            qi = p // 4
            uT = uTB[p % 2]
            qkT = qkTB[p % 2]
            Vp = VpB[p % 2]
            gt = gtB[p % 2]
            Gm = GmB[p % 2]
            def v3(c0):
                return sm[:, c0:c0 + 16].rearrange("p (a b) -> p a b", b=4)
            e1, nlf, t1, t2, ebv, ekv, ekL, dec = [v3(128 + 16 * i_) for i_ in range(8)]
            li = gt[:, :, 0:4]
            zf = gt[:, :, 4:8]
            act(e1, zf, AF.Exp, scale=-1.0)
            act(nlf, e1, AF.Ln, bias=1.0)
            yield
            pg = pss[3]
            nlf2 = sm[:, 128 + 16:128 + 32]
            mm(pg[:, 0:16], [(triF, nlf2)])
            mm(pg[:, 16:32], [(onesF, nlf2)])
            yield
            pgb = pg[:, 0:16].rearrange("p (a b) -> p a b", b=4)
            pgl = pg[:, 16:32].rearrange("p (a b) -> p a b", b=4)
            act(ebv, pgb, AF.Exp, scale=-1.0)
            stt(t1, li, LN_DK, pgb, ALU.add, ALU.add)
            act(ekv, t1, AF.Exp)
            tt('dve', t2, t1, pgl, ALU.subtract)
            act(ekL, t2, AF.Exp)
            act(dec, pgl, AF.Exp, scale=-1.0)
            yield
            for rt in range(4):
                q = qi * 4 + rt
                cols = slice(rt * 128, (rt + 1) * 128)
                if isq:
                    pS2 = pss[7]
                    for h in range(4):
                        mm(pS2[:, h * 128:(h + 1) * 128], [(qkT[:, 4 + h, cols], qkT[:, h, cols])])
                    yield
                    for h in range(4):
                        stt(ATB[h][:], pS2[:, h * 128:(h + 1) * 128], ekv[:, rt, h:h + 1], c01b[:], ALU.mult, ALU.mult)
                    yield
                    for h in range(4):
                        qT_h = qkT[:, h, cols]
                        pN = pss[6]

                        def fnN(e, h=h, rt=rt, qT_h=qT_h, pN=pN, Vp=Vp):
                            e.matmul(pN[:, 0:257], lhsT=ATB[h][:], rhs=Vp[:, rt, h, :], start=True, stop=False)
                            return e.matmul(pN[:, 0:257], lhsT=qT_h, rhs=Cb[:, h, :], start=False, stop=True)
                        P.add('pe', fnN, reads=[ATB[h][:], Vp[:, rt, h, :], qT_h, Cb[:, h, :]], writes=[pN[:, 0:257]])
                        yield
                        tt('dve', d1[:, h:h + 1], pN[:, 256:257], ebv[:, rt, h:h + 1], ALU.mult)
                        stt(d1[:, h:h + 1], d1[:, h:h + 1], -1.0, d1[:, h:h + 1], ALU.mult, ALU.max)
                        tsc('dve', d1[:, h:h + 1], d1[:, h:h + 1], 1.0, None, ALU.max)
                        P.add('dve', lambda e, h=h: e.reciprocal(d1[:, h:h + 1], d1[:, h:h + 1]), reads=[d1[:, h:h + 1]], writes=[d1[:, h:h + 1]])
                        tt('dve', scv[:, h:h + 1], d1[:, h:h + 1], ebv[:, rt, h:h + 1], ALU.mult)
                        tsc('dve', hh[:, h, :], pN[:, 0:256], scv[:, h:h + 1], None, ALU.mult)
                        act(junkb[:, 0:256], hh[:, h, :], AF.Square, accum=hs[:, h:h + 1])
                        yield
                pKv = pss[3][:].bitcast(BF16)[:, 512:1024].rearrange("p (a b) -> p a b", b=128)
                for h in range(4):
                    tr(pKv[:, h, :], qkT[:, 4 + h, cols], identb[:])
                yield
                for h in range(4):
                    tsc('dve', KhatB[h][:], pKv[:, h, :], ekL[:, rt, h:h + 1], None, ALU.mult)
                yield
                for h in range(4):
                    pC = pss[4 + (h % 2)]
                    mm(pC[:, 0:257], [(KhatB[h][:], Vp[:, rt, h, :])])
                    if h % 2 == 1:
                        yield
                        for h2 in (h - 1, h):
                            pC2 = pss[4 + (h2 % 2)]
                            stt(Cst[:, h2, :], Cst[:, h2, :], dec[:, rt, h2:h2 + 1], pC2[:, 0:257], ALU.mult, ALU.add)
                            cp('pool', Cb[:, h2, :], Cst[:, h2, :])
                        yield
                if isq:
                    tsc('dve', rn, hs, 1.0 / 256, 1e-6, ALU.mult, ALU.add)
                    act(rn, rn, AF.Sqrt)
                    P.add('dve', lambda e: e.reciprocal(rn, rn), reads=[rn], writes=[rn])
                    for h in range(4):
                        stt(hh[:, h, :], hh[:, h, :], rn[:, h:h + 1], mnb[:, h * 256:(h + 1) * 256], ALU.mult, ALU.mult)
                    tt('pool', hmg[:], hh[:].rearrange("p a b -> p (a b)"), Gm[:, rt, :], ALU.mult)
                    dma(HM_s[q], hmg[:])
                    yield

        def interleave(gens):
            while gens:
                gens.sort(key=lambda x: x[1] / x[2])
                gq = gens[0]
                try:
                    next(gq[0])
                    gq[1] += 1
                except StopIteration:
                    gens.remove(gq)

        for _ in genN(0):
            pass
        for step in range(NPOS + 1):
            gens = []
            if step + 1 < NPOS:
                gens.append([genN(step + 1), 0, 12])
            if step < NPOS:
                gens.append([genP(step), 0, 8 + 4 + 1 + (8 if step % 4 == 3 else 0) + 8 + 4 + 1 + (21 if step % 4 == 3 else 0)])
            if step == 1:
                gens.append([gen0_rest(), 0, 8])
            if step >= 1:
                pm_ = step - 1
                gens.append([genM(pm_), 0, int(0.75 * (3 + 4 * (6 + (11 if pm_ % 4 == 3 else 0))))])
            interleave(gens)

        bump[0] = 0
        scoresB = [carve([128, 8192], F32) for _ in range(2)]
        sel = carve([128, 8192], BF16)
        selTB = [carve([128, 64, 128], BF16) for _ in range(2)]
        rl = [carve([128, 512], F32) for _ in range(2)]
        kc = [carve([128, 4, 512], BF16) for _ in range(2)]
        vc = [carve([128, 4, 520], BF16) for _ in range(2)]
        Pm = [carve([128, 2, 256], BF16) for _ in range(3)]
        Gt = carve([128, 256, 8], F32)
        vmb = carve([128, 1536], F32)
        GtB = carve([128, 2, 8, 128], BF16)
        aqtB = [carve([128, 4, 256], BF16) for _ in range(2)]
        iqtB = [carve([128, 8, 128], BF16) for _ in range(2)]
        gatB = [carve([128, 512], BF16) for _ in range(2)]
        hatB = [carve([128, 512], BF16) for _ in range(2)]
        wqtB = [carve([128, 8], F32) for _ in range(2)]
        print("phase B arena bytes", bump[0])
        dma(vmb[:], vmask.partition_broadcast(128))
        for i_ in range(2):
            memset('pool', iqtB[i_][:], 0.0)
            memset('pool', aqtB[i_][:], 0.0)
        for s_ in range(128):
            dma(Gt[s_:s_ + 1, :, :], FV_s[128 - s_:128 - s_ + 256, :].rearrange("(o a) h -> o a h", o=1))
        for j_ in range(2):
            for h_ in range(8):
                cp('pool', GtB[:, j_, h_, :], Gt[:, j_ * 128:(j_ + 1) * 128, h_])
        cnt_i = [0]
        cnt_a = [0]
        kvi = [0]

        def geom(q):
            g = q // 4
            r4 = q % 4
            R = 4 * (4 * g + 3) + r4
            return R, (R + 1) * 128, (R + 4) // 4

        def skewed(unit_stages, nst):
            n = len(unit_stages)
            for step in range(n + nst - 1):
                for st_ in range(nst - 1, -1, -1):
                    u = step - st_
                    if 0 <= u < n:
                        unit_stages[u][st_]()
                yield

        def gen_idx(q):
            R, NK, nch = geom(q)
            scores = scoresB[q % 2]
            iqt = iqtB[q % 2]
            wqt = wqtB[q % 2]
            for hp in range(2):
                dma(iqt[hp * 64:(hp + 1) * 64, hp::2, :], IQ_s[hp * 64:(hp + 1) * 64, :, q * 128:(q + 1) * 128])
            dma(wqt[:], WQ_s[q])
            us = []
            for h in range(8):
                pr = h // 2
                ro = (h % 2) * 64
                for c in range(nch):
                    n = min(512, NK - c * 512)
                    ui = cnt_i[0]
                    cnt_i[0] += 1
                    pb = pss[1 + ui % 2]
                    rr_ = rl[ui % 2]
                    sc = scores[:, c * 512:c * 512 + n]

                    def s0(pb=pb, n=n, h=h, c=c):
                        mm(pb[:, 0:n], [(iqt[:, h, :], ikT[:, c * 512:c * 512 + n])])

                    def s1(pb=pb, n=n, rr_=rr_):
                        act(rr_[:, 0:n], pb[:, 0:n], AF.Relu)

                    def s2(rr_=rr_, n=n, sc=sc, h=h):
                        if h == 0:
                            tsc('dve', sc, rr_[:, 0:n], wqt[:, 0:1], None, ALU.mult)
                        else:
                            stt(sc, rr_[:, 0:n], wqt[:, h:h + 1], sc, ALU.mult, ALU.add)
                    us.append((s0, s1, s2))
            for _ in skewed(us, 3):
                yield
            tt('dve', scores[:, R * 128:(R + 1) * 128], scores[:, R * 128:(R + 1) * 128], cnegF, ALU.add)
            tt('pool', scores[:, 0:1536], scores[:, 0:1536], vmb[:], ALU.add)
            yield

        def gen_bis(q):
            R, NK, nch = geom(q)
            scores = scoresB[q % 2]
            selT = selTB[q % 2]
            lo = sm[:, 96 + 8 * (q % 2):97 + 8 * (q % 2)]
            mid = sm[:, 97 + 8 * (q % 2):98 + 8 * (q % 2)]
            cnt = sm[:, 98 + 8 * (q % 2):99 + 8 * (q % 2)]
            stp = sm[:, 99 + 8 * (q % 2):100 + 8 * (q % 2)]
            nmid = sm[:, 100 + 8 * (q % 2):101 + 8 * (q % 2)]
            sA = sm[:, 101 + 8 * (q % 2):102 + 8 * (q % 2)]
            Tt = sm[:, 102 + 8 * (q % 2):103 + 8 * (q % 2)]
            NA = int(round(ACT_FRAC * NK / 128.0)) * 128
            memset('dve', mid, 0.0)
            for it in range(NBIS):
                hw = 2048.0 / (2 ** (it + 1))
                hw_next = hw / 2.0 if it + 1 < NBIS else hw
                act(sel[:, 0:NA], scores[:, 0:NA], AF.Sign, bias=mid, scale=-1.0, accum=sA)
                P.add('dve', lambda e, NK=NK, NA=NA, scores=scores, mid=mid, cnt=cnt: e.tensor_scalar(sel[:, NA:NK], scores[:, NA:NK], mid, None, ALU.is_ge, ALU.add, accum_out=cnt),
                      reads=[scores[:, NA:NK], mid], writes=[sel[:, NA:NK], cnt])
                stt(Tt, cnt, 2.0, sA, ALU.mult, ALU.subtract)
                tsc('dve', stp, Tt, 511.0 - NA, hw, ALU.is_ge, ALU.mult)
                stt(mid, stp, -hw_next, mid, ALU.add, ALU.add)
                yield
            lo = mid
            tsc('dve', sel[:, 0:NK], scores[:, 0:NK], lo, None, ALU.is_lt)
            yield
            for kb0 in range(0, R + 1, 8):
                nb_ = min(8, R + 1 - kb0)
                pT = psb(0, [128, 8, 128])
                for i_ in range(nb_):
                    tr(pT[:, i_, :], sel[:, (kb0 + i_) * 128:(kb0 + i_ + 1) * 128], identb[:])
                cp('act', selT[:, kb0:kb0 + nb_, :], pT[:, 0:nb_, :])
                yield

        def gen_att(q):
            R, NK, nch = geom(q)
            selT = selTB[q % 2]
            aqt = aqtB[q % 2]
            gat = gatB[q % 2]
            hat = hatB[q % 2]
            rd8 = sm[:, 112 + 8 * (q % 2):120 + 8 * (q % 2)]
            dma(aqt[0:64, :, 0:128], AQ_s[0:64, :, q * 128:(q + 1) * 128])
            dma(aqt[64:128, :, 128:256], AQ_s[64:128, :, q * 128:(q + 1) * 128])
            dma(gat[:], GA_s[q])
            pO = [pss[5], pss[6]]
            us = []
            for c in range(nch):
                nb_ = min(4, R + 1 - 4 * c)
                kc_ = kc[kvi[0] % 2]
                vc_ = vc[kvi[0] % 2]
                kvi[0] += 1
                for pr in range(4):
                    for bh in range((nb_ + 1) // 2):
                        blks = [b_ for b_ in (2 * bh, 2 * bh + 1) if b_ < nb_]
                        ui = cnt_a[0]
                        cnt_a[0] += 1
                        pS = (pss[3], pss[4])[ui % 2]
                        pm = Pm[ui % 3]

                        def s0(c=c, pr=pr, bh=bh, blks=blks, nb_=nb_, kc_=kc_, vc_=vc_, pS=pS):
                            if pr == 0 and bh == 0:
                                dma(kc_[:, :, 0:nb_ * 128], KT_s[:, :, c * 512:c * 512 + nb_ * 128].rearrange("a d s -> d a s"))
                                dma(vc_[:, 0:nb_, :], VA_s[4 * c:4 * c + nb_].rearrange("r t c -> t r c"))
                            for i_, b_ in enumerate(blks):
                                def fnQ(e, i_=i_, b_=b_):
                                    kb = 4 * c + b_
                                    e.matmul(pS[:, i_ * 256:(i_ + 1) * 256], lhsT=kc_[:, pr, b_ * 128:(b_ + 1) * 128], rhs=aqt[:, pr, :], start=True, stop=False)
                                    if kb == R or kb == R - 1:
                                        j_ = 0 if kb == R else 1
                                        e.matmul(pS[:, i_ * 256:(i_ + 1) * 256], lhsT=identb[:], rhs=GtB[:, j_, 2 * pr:2 * pr + 2, :], start=False, stop=False)
                                    e.matmul(pS[:, i_ * 256:i_ * 256 + 128], lhsT=negI[:], rhs=selT[:, 4 * c + b_, :], start=False, stop=False)
                                    return e.matmul(pS[:, i_ * 256 + 128:(i_ + 1) * 256], lhsT=negI[:], rhs=selT[:, 4 * c + b_, :], start=False, stop=True)
                                P.add('pe', fnQ, reads=[kc_[:, pr, b_ * 128:(b_ + 1) * 128], aqt[:, pr, :], selT[:, 4 * c + b_, :], negI[:], identb[:], GtB[:]],
                                      writes=[pS[:, i_ * 256:(i_ + 1) * 256]])

                        def s1(c=c, pr=pr, blks=blks, pS=pS, pm=pm):
                            n_ = len(blks)
                            act(pm[:, 0:n_, :], pS[:, 0:n_ * 256].rearrange("p (a b) -> p a b", b=256), AF.Exp)

                        def s3(c=c, pr=pr, blks=blks, nb_=nb_, pm=pm, vc_=vc_):
                            for hp in range(2):
                                hh_ = 2 * pr + hp
                                po = pO[hh_ // 4]
                                oc = (hh_ % 4) * 65

                                def fnO(e, hp=hp, hh_=hh_, po=po, oc=oc):
                                    ins = None
                                    for i_, b_ in enumerate(blks):
                                        ins = e.matmul(po[:, oc:oc + 65], lhsT=pm[:, i_, hp * 128:(hp + 1) * 128], rhs=vc_[:, b_, hh_ * 65:(hh_ + 1) * 65],
                                                       start=(c == 0 and b_ == 0 and hh_ % 4 == 0), stop=(c == nch - 1 and b_ == nb_ - 1),
                                                       skip_group_check=True)
                                    return ins
                                P.add('pe', fnO, reads=[pm[:, 0:len(blks), hp * 128:(hp + 1) * 128], vc_[:, blks[0]:blks[-1] + 1, hh_ * 65:(hh_ + 1) * 65]], writes=[po[:, oc:oc + 65]])
                        us.append((s0, s1, s3))
            for _ in skewed(us, 3):
                yield
            for hf in range(2):
                po = pO[hf]
                P.add('dve', lambda e, po=po, hf=hf, rd8=rd8: e.reciprocal(rd8[:, hf * 4:hf * 4 + 4], po[:, 64:260:65]),
                      reads=[po[:, 0:260]], writes=[rd8[:, hf * 4:hf * 4 + 4]])
                for h4 in range(4):
                    h = hf * 4 + h4
                    stt(hat[:, h * 64:(h + 1) * 64], po[:, h4 * 65:h4 * 65 + 64], rd8[:, h:h + 1], gat[:, h * 64:(h + 1) * 64], ALU.mult, ALU.mult)
            dma(HA_s[q], hat[:])
            yield

        def units(kind, q):
            R, NK, nch = geom(q)
            if kind == 0:
                return 8 * nch + 3
            if kind == 1:
                return NBIS + 1 + (R + 8) // 8
            return 8 * nch + 4

        for step in range(NQT + 2):
            gens = []
            for kind, qq in ((0, step), (1, step - 1), (2, step - 2)):
                if 0 <= qq < min(NQT, QLIM):
                    gf = (gen_idx, gen_bis, gen_att)[kind]
                    gens.append([gf(qq), 0, units(kind, qq)])
            while gens:
                gens.sort(key=lambda x: x[1] / x[2])
                gq = gens[0]
                try:
                    next(gq[0])
                    gq[1] += 1
                except StopIteration:
                    gens.remove(gq)

        bump[0] = 0
        wo = carve([128, 12, 1024], BF16)
        wg = carve([128, 8, 1024], BF16)
        wp = carve([128, 2, 1024], BF16)
        stg = [carve([128, 1024], F32) for _ in range(2)]
        nfb = carve([128, 1024], F32)
        CB = []
        for i_ in range(3):
            CB.append(dict(
                cat=carve([128, 1536], BF16), catT=carve([128, 12, 128], BF16), xres=carve([128, 1024], F32),
                h1=carve([128, 1024], F32), h1b=carve([128, 1024], BF16), h1T=carve([128, 8, 128], BF16),
                pt=carve([128, 256], F32), ptb=carve([128, 256], BF16), ptT=carve([128, 2, 128], BF16),
                sg=carve([128, 1024], F32), ot=carve([128, 1024], F32), junk=carve([128, 1024], BF16)))
        print("phase C arena bytes", bump[0])
        dma(nfb[:], norm_final.partition_broadcast(128))
        si = [0]

        def load_w(dst, src_row0):
            s_ = stg[si[0] % 2]
            si[0] += 1
            dma(s_[:], src_row0)
            cp('dve' if si[0] % 2 else 'pool', dst, s_[:])
        for k in range(12):
            load_w(wo[:, k, :], w_out[k * 128:(k + 1) * 128, :])
        for k in range(8):
            load_w(wg[:, k, :], w_gate[k * 128:(k + 1) * 128, :])
        for k in range(2):
            load_w(wp[:, k, :], w_ple[k * 128:(k + 1) * 128, :])
        cbank = [0]

        def cbk():
            cbank[0] += 1
            return pss[(1, 2, 4, 5)[cbank[0] % 4]]

        def genC(q):
            B_ = CB[q % 3]
            cat, catT, xres, h1, h1b, h1T = B_['cat'], B_['catT'], B_['xres'], B_['h1'], B_['h1b'], B_['h1T']
            pt_, ptb, ptT, sg, ot, junkc = B_['pt'], B_['ptb'], B_['ptT'], B_['sg'], B_['ot'], B_['junk']
            ss2 = sm[:, 120 + 2 * (q % 3):121 + 2 * (q % 3)]
            rs2 = sm[:, 121 + 2 * (q % 3):122 + 2 * (q % 3)]
            pbT = (0, 3)[q % 2]
            pbT2 = (7, 6)[q % 2]
            g = q // 4
            r4 = q % 4
            r = 4 * (4 * g + 3) + r4
            dma(cat[:, 0:1024], HM_s[q])
            dma(cat[:, 1024:1536], HA_s[q])
            dma(xres[:], xk[r * 128:(r + 1) * 128, :])
            dma(pt_[:], pq[q * 128:(q + 1) * 128, :])
            yield
            for k0 in (0, 8):
                nb_ = 8 if k0 == 0 else 4
                pT = psb(pbT, [128, 8, 128])
                for i_ in range(nb_):
                    tr(pT[:, i_, :], cat[:, (k0 + i_) * 128:(k0 + i_ + 1) * 128], identb[:])
                yield
                cp('act', catT[:, k0:k0 + nb_, :], pT[:, 0:nb_, :])
                yield
            cp('pool', ptb[:], pt_[:])
            for half in range(2):
                pb = cbk()
                mm(pb[:], [(catT[:, k, :], wo[:, k, half * 512:(half + 1) * 512]) for k in range(12)])
                yield
                tt('dve', h1[:, half * 512:(half + 1) * 512], pb[:], xres[:, half * 512:(half + 1) * 512], ALU.add)
                yield
            cp('pool', h1b[:], h1[:])
            yield
            pT = psb(pbT, [128, 8, 128])
            for k in range(8):
                tr(pT[:, k, :], h1b[:, k * 128:(k + 1) * 128], identb[:])
            pT2 = psb(pbT2, [128, 2, 128])
            for k in range(2):
                tr(pT2[:, k, :], ptb[:, k * 128:(k + 1) * 128], identb[:])
            yield
            cp('act', h1T[:], pT)
            cp('act', ptT[:], pT2)
            yield
            for half in range(2):
                hs_ = slice(half * 512, (half + 1) * 512)
                pb = cbk()
                mm(pb[:], [(h1T[:, k, :], wg[:, k, hs_]) for k in range(8)])
                pb2 = cbk()
                mm(pb2[:], [(ptT[:, k, :], wp[:, k, hs_]) for k in range(2)])
                yield
                act(sg[:, hs_], pb[:], AF.Sigmoid)
                yield
                tt('dve', sg[:, hs_], pb2[:], sg[:, hs_], ALU.mult)
                yield
            tt('pool', h1[:], h1[:], sg[:], ALU.add)
            yield
            act(junkc[:], h1[:], AF.Square, accum=ss2)
            yield
            tsc('dve', rs2, ss2, 1.0 / 1024, 1e-6, ALU.mult, ALU.add)
            act(rs2, rs2, AF.Sqrt)
            P.add('dve', lambda e, rs2=rs2: e.reciprocal(rs2, rs2), reads=[rs2], writes=[rs2])
            yield
            stt(ot[:], h1[:], rs2, nfb[:], ALU.mult, ALU.mult)
            dma(out[q * 128:(q + 1) * 128, :], ot[:])
            yield

        def roll(gen_list, lag):
            active = []
            nxt = 0
            started_at = []
            while nxt < len(gen_list) or active:
                if nxt < len(gen_list) and (not active or active[-1][1] >= lag):
                    active.append([gen_list[nxt], 0])
                    nxt += 1
                for a_ in list(active):
                    try:
                        next(a_[0])
                        a_[1] += 1
                    except StopIteration:
                        active.remove(a_)
        roll([genC(q) for q in range(min(NQT, QLIM))], 8)

        P.prepare()
        print("ops:", {e: len(P.ops[e]) for e in Prog.ENGS})
        with nc.Block() as block:
            @block.sync
            def _(e):
                P.run('sp', e, sems, dsems)
                P.final_waits(e, dsems)

            @block.scalar
            def _(e):
                P.run('act', e, sems, dsems)

            @block.vector
            def _(e):
                P.run('dve', e, sems, dsems)

            @block.gpsimd
            def _(e):
                P.run('pool', e, sems, dsems)

            @block.tensor
            def _(e):
                P.run('pe', e, sems, dsems)
    return nc


def make_core_inputs(inputs):
    x = np.asarray(inputs["x"], np.float32)
    p = np.asarray(inputs["p"], np.float32)[0]
    consts = make_consts()
    shared = {
        "w_in": np.ascontiguousarray(np.asarray(inputs["w_in"], np.float32)[0]),
        "b_gate": np.ascontiguousarray(np.asarray(inputs["b_gate"], np.float32)[0:1]),
        "conv_w": np.ascontiguousarray(np.asarray(inputs["conv_w"], np.float32)[0].reshape(4, 8, 128).transpose(2, 0, 1).reshape(128, 32)),
        "w_out": np.ascontiguousarray(np.asarray(inputs["w_out"], np.float32)[0]),
        "norm_in": np.ascontiguousarray(np.asarray(inputs["norm_in"], np.float32)[0].reshape(8, 128).T),
        "m_norm": np.ascontiguousarray(np.asarray(inputs["m_norm"], np.float32)[0:1]),
        "rel_bias": np.ascontiguousarray(np.asarray(inputs["rel_bias"], np.float32)),
        "w_ple": np.ascontiguousarray(np.asarray(inputs["w_ple"], np.float32)[0]),
        "w_ple_gate": np.ascontiguousarray(np.asarray(inputs["w_ple_gate"], np.float32)[0]),
        "norm_final": np.ascontiguousarray(np.asarray(inputs["norm_final"], np.float32).reshape(1, D)),
        "consts": consts,
    }
    maps = []
    for c in range(8):
        b, j = c // 4, c % 4
        xkc = np.zeros((8192, D), np.float32)
        n0 = 512 * (3 - j)
        xkc[n0:] = x[b, 0:8192 - n0]
        pqc = np.zeros((2048, 256), np.float32)
        for g in range(4):
            ch = 4 * g + j
            pqc[g * 512:(g + 1) * 512] = p[b, ch * 512:(ch + 1) * 512]
        vm = np.zeros((1, 1536), np.float32)
        vm[0, 0:n0] = -BIG
        m = dict(shared)
        m["xk"] = xkc
        m["pq"] = pqc
        m["vmask"] = vm
        maps.append(m)
    return maps


_NC_CACHE = {}


def kernel(**inputs):
    maps = make_core_inputs(inputs)
    if 'nc' not in _NC_CACHE:
        _NC_CACHE['nc'] = build_program(DEBUG)
    nc = _NC_CACHE['nc']
    res = run_bass_kernel_spmd(nc, maps, core_ids=list(range(8)))
    outp = np.zeros((2, 8192, D), np.float32)
    for c in range(8):
        b, j = c // 4, c % 4
        o = np.asarray(res.results[c]["out"], np.float32)
        for g in range(4):
            ch = 4 * g + j
            outp[b, ch * 512:(ch + 1) * 512] = o[g * 512:(g + 1) * 512]
    if DEBUG:
        kernel.last = res
    return outp
```
